# Optimizing a Trainium2 kernel written in Bass

```python
import math
import jax, jax.numpy as jnp
from jax import lax
import numpy as np

D_MODEL = 1024
BATCH = 2
SEQ = 8192
DEPTH = 1

CHUNK = 64
BLOCK_Q = 128
SB_HEADS = 8
SB_HEAD_DIM = 64
SB_WIDTH = SB_HEADS * SB_HEAD_DIM
DIFF_HEADS = 4
DIFF_HEAD_DIM = 64
DIFF_WIDTH = DIFF_HEADS * 2 * DIFF_HEAD_DIM
REL_BUCKETS = 32
REL_MAX_DIST = 128
D_FF = 4 * D_MODEL
N_BRANCHES = 2
IN_COLS = 3 * SB_WIDTH + 3 * DIFF_WIDTH + N_BRANCHES * D_MODEL
NORM_EPS = 1e-6

kernel_name = "hybrid_stickbreak_diffattn_block"


def rms_norm(x, g):
    xf = x.astype(jnp.float32)
    y = xf * lax.rsqrt(jnp.mean(xf * xf, axis=-1, keepdims=True) + NORM_EPS)
    return (y * g.astype(jnp.float32)).astype(x.dtype)


def t5_bucket(rel):
    half = REL_BUCKETS // 2
    max_exact = half // 2
    ret = jnp.where(rel > 0, half, 0)
    n = jnp.abs(rel)
    nf = jnp.maximum(n, 1).astype(jnp.float32)
    large = max_exact + (jnp.log(nf / max_exact) / math.log(REL_MAX_DIST / max_exact)
                         * (half - max_exact)).astype(jnp.int32)
    large = jnp.minimum(large, half - 1)
    return ret + jnp.where(n < max_exact, n, large)


def to_blocks(t):
    b, s, h, d = t.shape
    return t.reshape(b, s // BLOCK_Q, BLOCK_Q, h, d).transpose(1, 0, 3, 2, 4)


def from_blocks(t):
    nb, b, h, q, d = t.shape
    return t.transpose(1, 0, 3, 2, 4).reshape(b, nb * q, h * d)


def stick_breaking_attention(q, k, v):
    seq = q.shape[1]
    scale = SB_HEAD_DIM ** -0.5
    kh = k.transpose(0, 2, 1, 3)
    vh = v.transpose(0, 2, 1, 3)
    k_pos = jnp.arange(seq, dtype=jnp.int32)
    q_pos = k_pos.reshape(seq // BLOCK_Q, BLOCK_Q)

    def block(args):
        qb, qp = args
        z = jnp.einsum('bhqd,bhkd->bhqk', qb, kh).astype(jnp.float32) * scale
        causal = k_pos[None, :] < qp[:, None]
        log_beta = jax.nn.log_sigmoid(z)
        log_one_minus = jnp.where(causal, jax.nn.log_sigmoid(-z), 0.0)
        shifted = jnp.concatenate([log_one_minus[..., 1:], jnp.zeros_like(log_one_minus[..., :1])], axis=-1)
        tail = lax.cumsum(shifted, axis=shifted.ndim - 1, reverse=True)
        w = jnp.where(causal, jnp.exp(log_beta + tail), 0.0)
        return jnp.einsum('bhqk,bhkd->bhqd', w.astype(vh.dtype), vh)

    out = lax.map(block, (to_blocks(q), q_pos))
    return from_blocks(out)


def differential_attention(q1, q2, k1, k2, v, lam, rel_bias):
    seq = q1.shape[1]
    scale = DIFF_HEAD_DIM ** -0.5
    k1h = k1.transpose(0, 2, 1, 3)
    k2h = k2.transpose(0, 2, 1, 3)
    vh = v.transpose(0, 2, 1, 3)
    k_pos = jnp.arange(seq, dtype=jnp.int32)
    q_pos = k_pos.reshape(seq // BLOCK_Q, BLOCK_Q)
    neg = jnp.finfo(jnp.float32).min
    table = rel_bias.astype(jnp.float32)

    def block(args):
        q1b, q2b, qp = args
        allowed = k_pos[None, :] < (qp[:, None] // CHUNK + 1) * CHUNK
        bias = jnp.transpose(table[t5_bucket(k_pos[None, :] - qp[:, None])], (2, 0, 1))

        def probs(qb, kh):
            s = jnp.einsum('bhqd,bhkd->bhqk', qb, kh).astype(jnp.float32) * scale + bias
            return jax.nn.softmax(jnp.where(allowed, s, neg), axis=-1)

        a = probs(q1b, k1h) - lam * probs(q2b, k2h)
        return jnp.einsum('bhqk,bhkd->bhqd', a.astype(vh.dtype), vh)

    out = lax.map(block, (to_blocks(q1), to_blocks(q2), q_pos))
    nb, b, h, q, d = out.shape
    return out.transpose(1, 0, 3, 2, 4).reshape(b, nb * q, h, d)


def setup_inputs(seed: int = 0) -> dict:
    key = jax.random.key(seed)
    ks = jax.random.split(key, 20)
    f32 = jnp.float32

    def w(k, shape, fan_in):
        return jax.random.normal(k, shape, f32) * (fan_in ** -0.5)

    def gain(k, shape):
        return 1.0 + 0.02 * jax.random.normal(k, shape, f32)

    return {
        "x": jax.random.normal(ks[0], (BATCH, SEQ, D_MODEL), f32),
        "w_in": w(ks[1], (DEPTH, D_MODEL, IN_COLS), D_MODEL),
        "w_sb_out": w(ks[2], (DEPTH, SB_WIDTH, D_MODEL), SB_WIDTH),
        "w_diff_out": w(ks[3], (DEPTH, DIFF_WIDTH, D_MODEL), DIFF_WIDTH),
        "w_o": w(ks[4], (DEPTH, D_MODEL, D_MODEL), D_MODEL),
        "lambda_q1": 0.1 * jax.random.normal(ks[5], (DEPTH, DIFF_HEAD_DIM), f32),
        "lambda_k1": 0.1 * jax.random.normal(ks[6], (DEPTH, DIFF_HEAD_DIM), f32),
        "lambda_q2": 0.1 * jax.random.normal(ks[7], (DEPTH, DIFF_HEAD_DIM), f32),
        "lambda_k2": 0.1 * jax.random.normal(ks[8], (DEPTH, DIFF_HEAD_DIM), f32),
        "w_subln": gain(ks[9], (DEPTH, 2 * DIFF_HEAD_DIM)),
        "rel_bias": 0.5 * jax.random.normal(ks[10], (REL_BUCKETS, DIFF_HEADS), f32),
        "g_pre_mix": gain(ks[11], (DEPTH, D_MODEL)),
        "g_post_mix": gain(ks[12], (DEPTH, D_MODEL)),
        "g_pre_mlp": gain(ks[13], (DEPTH, D_MODEL)),
        "g_post_mlp": gain(ks[14], (DEPTH, D_MODEL)),
        "w_up": w(ks[15], (DEPTH, D_MODEL, D_FF), D_MODEL),
        "w_down": w(ks[16], (DEPTH, D_FF, D_MODEL), D_FF),
    }


def reference(x, w_in, w_sb_out, w_diff_out, w_o, lambda_q1, lambda_k1, lambda_q2, lambda_k2,
              w_subln, rel_bias, g_pre_mix, g_post_mix, g_pre_mlp, g_post_mlp, w_up, w_down):
    b, s, _ = x.shape
    for l in range(DEPTH):
        h = rms_norm(x, g_pre_mix[l])
        proj = h @ w_in[l]
        c0 = 3 * SB_WIDTH
        c1 = c0 + 3 * DIFF_WIDTH
        sb_q, sb_k, sb_v = jnp.split(proj[..., :c0], 3, axis=-1)
        d_q, d_k, d_v = jnp.split(proj[..., c0:c1], 3, axis=-1)
        gates = jax.nn.sigmoid(proj[..., c1:].astype(jnp.float32)).astype(x.dtype)
        gate_sb, gate_diff = jnp.split(gates, N_BRANCHES, axis=-1)

        sb_shape = (b, s, SB_HEADS, SB_HEAD_DIM)
        y_sb = stick_breaking_attention(sb_q.reshape(sb_shape), sb_k.reshape(sb_shape), sb_v.reshape(sb_shape))

        d_q = d_q.reshape(b, s, DIFF_HEADS, 2, DIFF_HEAD_DIM)
        d_k = d_k.reshape(b, s, DIFF_HEADS, 2, DIFF_HEAD_DIM)
        d_v = d_v.reshape(b, s, DIFF_HEADS, 2 * DIFF_HEAD_DIM)
        lam_init = 0.8 - 0.6 * math.exp(-0.3 * l)
        lam = (jnp.exp(jnp.sum(lambda_q1[l].astype(jnp.float32) * lambda_k1[l].astype(jnp.float32)))
               - jnp.exp(jnp.sum(lambda_q2[l].astype(jnp.float32) * lambda_k2[l].astype(jnp.float32)))
               + lam_init)
        y_diff = differential_attention(d_q[..., 0, :], d_q[..., 1, :], d_k[..., 0, :], d_k[..., 1, :],
                                        d_v, lam, rel_bias)
        y_diff = (rms_norm(y_diff, w_subln[l]) * (1.0 - lam_init)).reshape(b, s, DIFF_WIDTH)

        merged = gate_sb * (y_sb @ w_sb_out[l]) + gate_diff * (y_diff @ w_diff_out[l])
        x = x + rms_norm(merged @ w_o[l], g_post_mix[l])

        h = rms_norm(x, g_pre_mlp[l])
        u = jnp.square(jax.nn.relu(h @ w_up[l]))
        x = x + rms_norm(u @ w_down[l], g_post_mlp[l])
    return x
```

```python
import math
import numpy as np
import ml_dtypes
import concourse.bass as bass
import concourse.mybir as mybir
from concourse.bass_utils import run_bass_kernel_spmd

F32 = mybir.dt.float32
BF16 = mybir.dt.bfloat16
AF = mybir.ActivationFunctionType
ALU = mybir.AluOpType
AX = mybir.AxisListType

SEQ = 8192
D = 1024
DFF = 4096
NQT = 16
NKB = 64
OWN = 2048
EPS = 1e-6
NEG = -30000.0
LAM_INIT = 0.8 - 0.6 * math.exp(-0.3 * 0)
ENGS = ("pe", "act", "dve", "pool", "sp")


class Op:
    __slots__ = ("eng", "fn", "deps", "is_async", "inc", "sem", "val", "signaled", "noinline")

    def __init__(self, eng, fn, is_async, inc):
        self.eng = eng
        self.fn = fn
        self.deps = []
        self.is_async = is_async
        self.inc = inc
        self.sem = None
        self.val = 0
        self.signaled = is_async
        self.noinline = False


class Sched:
    def __init__(self, nc):
        self.nc = nc
        self.ops = {e: [] for e in ENGS}
        self.last_w = {}
        self.readers = {}
        self.akeys = {}

    def alias(self, new_keys, old_keys):
        olds = []
        seen = set()
        for k in old_keys:
            w = self.last_w.get(k)
            for o in ([w] if w is not None else []) + list(self.readers.get(k, ())):
                if id(o) not in seen:
                    seen.add(id(o))
                    olds.append(o)
        for k in new_keys:
            self.last_w[k] = None
            self.readers[k] = list(olds)

    def op(self, eng, fn, reads=(), writes=(), akey=None, inc=16, chain=False, noinline=False):
        o = Op(eng, fn, akey is not None, inc)
        o.noinline = noinline
        sem = ("a", akey) if akey is not None else None
        deps = []
        for k in reads:
            w = self.last_w.get(k)
            if w is not None:
                deps.append(w)
        for k in writes:
            w = self.last_w.get(k)
            if w is not None and not (chain and w.sem == sem):
                deps.append(w)
            deps.extend(self.readers.get(k, ()))
        seen = set()
        for d in deps:
            if id(d) in seen:
                continue
            seen.add(id(d))
            if (not d.is_async) and d.eng == eng and eng == "pe":
                continue
            o.deps.append(d)
            d.signaled = True
        if akey is not None:
            o.sem = sem
            self.akeys[akey] = self.akeys.get(akey, 0) + inc
            o.val = self.akeys[akey]
        for k in writes:
            self.last_w[k] = o
            self.readers[k] = []
        for k in reads:
            if k not in writes:
                lst = self.readers.setdefault(k, [])
                if not o.is_async:
                    lst[:] = [r for r in lst if r.is_async or r.eng != eng]
                lst.append(o)
        self.ops[eng].append(o)
        return o

    def emit(self):
        nc = self.nc
        for e in ENGS:
            c = 0
            for o in self.ops[e]:
                if o.is_async:
                    continue
                if o.signaled:
                    c += 1
                    o.sem = ("e", e)
                    o.val = c
        names = [("e", e) for e in ENGS] + [("a", k) for k in self.akeys]
        sems = {}
        for i, sn in enumerate(names):
            sems[sn] = nc.alloc_semaphore(name="s%d" % i)
        self.n_sems = len(names)
        final = {("a", k): v for k, v in self.akeys.items()}
        ops = self.ops

        def run_stream(e, engine):
            waited = {}
            for o in ops[e]:
                need = {}
                for d in o.deps:
                    if d.val > waited.get(d.sem, 0) and d.val > need.get(d.sem, 0):
                        need[d.sem] = d.val
                items = list(need.items())
                inl = None
                if items and e in ("act", "dve", "pool") and not o.is_async and not o.noinline:
                    inl = items.pop()
                for sm, v in items:
                    engine.wait_ge(sems[sm], v)
                    waited[sm] = v
                ins = o.fn(engine)
                if inl is not None:
                    ins.wait_op(sems[inl[0]], inl[1], "sem-ge")
                    waited[inl[0]] = inl[1]
                if o.is_async:
                    ins.then_inc(sems[o.sem], o.inc)
                elif o.signaled:
                    ins.then_inc(sems[o.sem], 1)
            if e == "sp":
                for k, v in final.items():
                    if waited.get(k, 0) < v:
                        engine.wait_ge(sems[k], v)

        with nc.Block() as block:
            @block.tensor
            def _(eng):
                run_stream("pe", eng)

            @block.scalar
            def _(eng):
                run_stream("act", eng)

            @block.vector
            def _(eng):
                run_stream("dve", eng)

            @block.gpsimd
            def _(eng):
                run_stream("pool", eng)

            @block.sync
            def _(eng):
                run_stream("sp", eng)


def build(stop_after=3, dbg=False, skip12=False, p3_limit=99):
    nc = bass.Bass("TRN2", target_bir_lowering=False)
    S = Sched(nc)

    def din(name, shape, dt=F32):
        return nc.dram_tensor(name, list(shape), dt, kind="ExternalInput").ap()

    xb = din("xb", [SEQ, D])
    xs = din("xs", [OWN, D])
    wqkv = din("wqkv", [D, 768])
    wgate = din("wgate", [D, 2 * D])
    wsb = din("wsb", [512, D])
    wdf = din("wdf", [512, D])
    wo = din("wo", [D, D])
    wup = din("wup", [D, DFF])
    wdn = din("wdn", [DFF, D])
    gvec = din("gvec", [4, D])
    lamv = din("lamv", [1, 256])
    subln = din("subln", [128, 1])
    btd = din("bt", [128, 5 * 512])
    c15d = din("c15", [128, 1])
    ohd = din("onehot", [128, 4])
    out = nc.dram_tensor("out", [OWN, D], F32, kind="ExternalOutput").ap()
    rs_in_l = [nc.dram_tensor("rs_in0", [4 * 1024, 1536], BF16), nc.dram_tensor("rs_in1", [4 * 1024, 512], BF16)]
    rs_out_l = [nc.dram_tensor("rs_out0", [1024, 1536], BF16), nc.dram_tensor("rs_out1", [1024, 512], BF16)]
    GROUPS = [[qt for qt in range(NQT) if qt % 4 != 3], [qt for qt in range(NQT) if qt % 4 == 3]]
    cur = {"gi": 0}
    if dbg:
        d_qk = nc.dram_tensor("d_qk", [4, 128, SEQ], BF16, kind="ExternalOutput").ap()
        d_v = nc.dram_tensor("d_v", [2, 128, NKB * 128], BF16, kind="ExternalOutput").ap()
        d_rs = nc.dram_tensor("d_rs", [1024, OWN], BF16, kind="ExternalOutput").ap()

    arena = nc.alloc_sbuf_tensor("arena", [128, 212832], mybir.dt.uint8)
    ABASE = 16512
    assert nc.sbuf_base >= ABASE + 212832 - 64, (nc.sbuf_base,)

    def sb(name, shape, dt, off):
        return nc.alloc_sbuf_tensor_at(name, list(shape), dt, offset=ABASE + off)

    ps = nc.alloc_psum_tensor("ps", [128, 8 * 512], F32)
    psb16 = ps.bitcast(BF16)

    def bank(i, lo=0, hi=512, p0=0, p1=128):
        return ps[p0:p1, i * 512 + lo:i * 512 + hi]

    def pk(i):
        return ("pb", i)

    R0, C0, T0 = 0, 98304, 100352
    QT_sb = sb("QT_sb", [128, SEQ], BF16, R0 + 0)
    KT_sb = sb("KT_sb", [128, SEQ], BF16, R0 + 16384)
    QT_d = sb("QT_d", [128, SEQ], BF16, R0 + 32768)
    QT_sbB = sb("QT_sbB", [128, SEQ], BF16, T0 + 77824)
    QT_dB = sb("QT_dB", [128, SEQ], BF16, T0 + 94208)
    QTs = [QT_sb, QT_sbB]
    QTd = [QT_d, QT_dB]
    KT_d = sb("KT_d", [128, SEQ], BF16, R0 + 49152)
    V_sb = sb("V_sb", [128, NKB, 128], BF16, R0 + 65536)
    V_d = sb("V_d", [128, NKB, 128], BF16, R0 + 81920)
    ident = sb("ident", [128, 128], BF16, C0 + 0)
    uneg = sb("uneg", [128, 128], BF16, C0 + 256)
    ones = sb("ones", [128, 128], BF16, C0 + 512)
    onesm = sb("onesm", [128, 128], BF16, C0 + 768)
    mtri = sb("mtri", [128, 128], BF16, C0 + 1024)
    onehot = sb("onehot_sb", [128, 4], F32, C0 + 1280)
    c15 = sb("c15_sb", [128, 1], F32, C0 + 1312)
    sublnS = sb("subln_sb", [128, 1], F32, C0 + 1344)
    lamS = sb("lam_sb", [128, 8], F32, C0 + 1376)
    stat = sb("stat", [128, 32], F32, C0 + 1408)
    cf32 = sb("cf32", [128, 128], F32, C0 + 1536)

    def acts(fn, reads, writes):
        return S.op("act", fn, reads, writes)

    def build_const(dst, val, sel):
        key = "cf32"
        S.op("pool", lambda e: e.memset(cf32[:, :], val), writes=[key])
        if sel is not None:
            S.op("pool", lambda e: e.affine_select(out=cf32[:, :], in_=cf32[:, :], pattern=[[-1, 128]],
                                                   compare_op=sel, fill=0.0, base=0, channel_multiplier=1),
                 reads=[key], writes=[key])
        S.op("dve", lambda e: e.tensor_copy(out=dst[:, :], in_=cf32[:, :]), reads=[key], writes=[dst.name])

    QZ = ["qz0", "qz1", "qz2", "qz3"]
    build_const(ident, 1.0, ALU.is_equal)
    build_const(uneg, -1.0, ALU.is_ge)
    build_const(mtri, NEG, ALU.is_ge)
    build_const(onesm, 1.0 / 128.0, None)
    build_const(ones, 1.0, None)

    xsl = [sb("xsl%d" % i, [128, D], F32, T0 + 4096 * i) for i in range(4)]
    junk = sb("junk", [128, D], BF16, T0 + 16384)
    xn = [sb("xn%d" % i, [128, D], BF16, T0 + 18432 + 2048 * i) for i in range(2)]
    hT = [sb("hT%d" % i, [128, 8, 512], BF16, T0 + 22528 + 8192 * i) for i in range(2)]
    wq = sb("wq", [128, 8, 768], BF16, T0 + 38912)
    gbc0 = sb("gbc0", [128, D], F32, T0 + 51200)
    lamB = sb("lamB", [128, 256], F32, T0 + 55296)
    lamP = sb("lamP", [128, 128], F32, T0 + 56320)
    P1_KEYS = ["xsl%d" % i for i in range(4)] + ["junk", "xn0", "xn1", "hT0", "hT1", "wq", "gbc0", "lamB", "lamP"] + ["xnq%d" % i for i in range(8)]

    cst = "consts"
    S.op("sp", lambda e: e.dma_start(out=onehot[:, :], in_=ohd[:, :]), writes=[cst], akey=cst, chain=True)
    S.op("sp", lambda e: e.dma_start(out=c15[:, :], in_=c15d[:, :]), writes=[cst], akey=cst, chain=True)
    S.op("sp", lambda e: e.dma_start(out=sublnS[:, :], in_=subln[:, :]), writes=[cst], akey=cst, chain=True)
    S.op("sp", lambda e: e.dma_start(out=gbc0[:, :], in_=gvec[0:1, :].partition_broadcast(128)), writes=[cst], akey=cst, chain=True)
    S.op("sp", lambda e: e.dma_start(out=lamB[:, :], in_=lamv[0:1, :].partition_broadcast(128)), writes=[cst], akey=cst, chain=True)
    S.op("pool", lambda e: e.dma_start(out=wq[:, :, :], in_=wqkv.rearrange("(kc p) n -> p kc n", p=128)),
         writes=["wq"], akey="wq")
    S.op("pool", lambda e: e.memset(QT_sb[64:128, :], 0.0), writes=["qz0"])
    S.op("pool", lambda e: e.memset(QT_sbB[0:64, :], 0.0), writes=["qz1"])
    S.op("pool", lambda e: e.memset(QT_d[64:128, :], 0.0), writes=["qz2"])
    S.op("pool", lambda e: e.memset(QT_dB[0:64, :], 0.0), writes=["qz3"])

    S.op("dve", lambda e: e.tensor_tensor(out=lamP[:, 0:64], in0=lamB[:, 0:64], in1=lamB[:, 64:128], op=ALU.mult), reads=[cst], writes=["lamP"])
    S.op("dve", lambda e: e.tensor_tensor(out=lamP[:, 64:128], in0=lamB[:, 128:192], in1=lamB[:, 192:256], op=ALU.mult), reads=[cst, "lamP"], writes=["lamP"])
    S.op("dve", lambda e: e.reduce_sum(out=lamS[:, 0:1], in_=lamP[:, 0:64], axis=AX.X), reads=["lamP"], writes=["lamS"])
    S.op("dve", lambda e: e.reduce_sum(out=lamS[:, 1:2], in_=lamP[:, 64:128], axis=AX.X), reads=["lamP", "lamS"], writes=["lamS"])
    S.op("act", lambda e: e.activation(out=lamS[:, 2:4], in_=lamS[:, 0:2], func=AF.Exp), reads=["lamS"], writes=["lamS"])
    S.op("dve", lambda e: e.tensor_tensor(out=lamS[:, 4:5], in0=lamS[:, 3:4], in1=lamS[:, 2:3], op=ALU.subtract), reads=["lamS"], writes=["lamS"])
    S.op("dve", lambda e: e.tensor_scalar(out=lamS[:, 5:6], in0=lamS[:, 4:5], scalar1=-LAM_INIT, scalar2=None, op0=ALU.add), reads=["lamS"], writes=["lamS"])
    S.op("dve", lambda e: e.tensor_scalar(out=sublnS[:, :], in0=sublnS[:, :], scalar1=1.0 - LAM_INIT, scalar2=None, op0=ALU.mult), reads=[cst], writes=["sublnS"])
    neglam = lamS[:, 5:6]

    cnt = {"stat": 0, "xn": 0}

    def norm_transpose(x_ap, xkey, gbc, gkeys, hT_t, hkey, s, tr_banks, nb=None):
        if nb is None:
            nb = (junk, "junk", xn, ["xn0", "xn1"])
        junk_t, junk_k, xn_l, xn_k = nb
        c = (cnt["stat"] % 8) * 4
        cnt["stat"] += 1
        skey = ("stat", c)
        S.op("act", lambda e: e.activation(out=junk_t[:, :], in_=x_ap, func=AF.Square, accum_out=stat[:, c:c + 1]),
             reads=[xkey], writes=[junk_k, skey], noinline=True)
        S.op("act", lambda e: e.activation(out=stat[:, c + 1:c + 2], in_=stat[:, c:c + 1], func=AF.Ln, scale=1.0 / D, bias=EPS),
             reads=[skey], writes=[skey])
        S.op("act", lambda e: e.activation(out=stat[:, c + 2:c + 3], in_=stat[:, c + 1:c + 2], func=AF.Exp, scale=-0.5),
             reads=[skey], writes=[skey])
        xi = cnt["xn"] % 2
        cnt["xn"] += 1
        xt = xn_l[xi]
        S.op("dve", lambda e: e.scalar_tensor_tensor(out=xt[:, :], in0=x_ap, scalar=stat[:, c + 2:c + 3], in1=gbc[:, :],
                                                     op0=ALU.mult, op1=ALU.mult),
             reads=[xkey, skey] + gkeys, writes=[xn_k[xi]])
        for kc in range(8):
            bi = tr_banks[kc // 2]
            o_ap = psb16[:, bi * 1024 + (kc % 2) * 512 + s * 128: bi * 1024 + (kc % 2) * 512 + (s + 1) * 128]
            S.op("pe", lambda e, o_ap=o_ap, kc=kc: e.transpose(o_ap, xt[:, kc * 128:(kc + 1) * 128], ident[:, :]),
                 reads=[xn_k[xi], "ident"], writes=[pk(bi)])

    def norm_only(x_ap, xkey, gbc, gkeys, xt, xk, junk_t, junk_k):
        c = (cnt["stat"] % 8) * 4
        cnt["stat"] += 1
        skey = ("stat", c)
        S.op("act", lambda e: e.activation(out=junk_t[:, :], in_=x_ap, func=AF.Square, accum_out=stat[:, c:c + 1]),
             reads=[xkey], writes=[junk_k, skey], noinline=True)
        S.op("act", lambda e: e.activation(out=stat[:, c + 1:c + 2], in_=stat[:, c:c + 1], func=AF.Ln, scale=1.0 / D, bias=EPS),
             reads=[skey], writes=[skey])
        S.op("act", lambda e: e.activation(out=stat[:, c + 2:c + 3], in_=stat[:, c + 1:c + 2], func=AF.Exp, scale=-0.5),
             reads=[skey], writes=[skey])
        S.op("dve", lambda e: e.scalar_tensor_tensor(out=xt[:, :], in0=x_ap, scalar=stat[:, c + 2:c + 3], in1=gbc[:, :],
                                                     op0=ALU.mult, op1=ALU.mult),
             reads=[xkey, skey] + gkeys, writes=[xk])

    def tr_only(xt, xk, s, tr_banks):
        for kc in range(8):
            bi = tr_banks[kc // 2]
            o_ap = psb16[:, bi * 1024 + (kc % 2) * 512 + s * 128: bi * 1024 + (kc % 2) * 512 + (s + 1) * 128]
            S.op("pe", lambda e, o_ap=o_ap, kc=kc: e.transpose(o_ap, xt[:, kc * 128:(kc + 1) * 128], ident[:, :]),
                 reads=[xk, "ident"], writes=[pk(bi)])

    def evac_transposes(hT_t, hkey, tr_banks):
        for i, bi in enumerate(tr_banks):
            src = psb16[:, bi * 1024:(bi + 1) * 1024]
            dst = hT_t[:, 2 * i:2 * i + 2, :]
            if i % 2 == 0:
                S.op("act", lambda e, src=src, dst=dst: e.copy(out=dst, in_=src.rearrange("p (a n) -> p a n", a=2)),
                     reads=[pk(bi)], writes=[hkey])
            else:
                S.op("dve", lambda e, src=src, dst=dst: e.tensor_copy(out=dst, in_=src.rearrange("p (a n) -> p a n", a=2)),
                     reads=[pk(bi)], writes=[hkey])

    xcnt = [0]

    def load_x(src_ap):
        i = xcnt[0] % 4
        xcnt[0] += 1
        S.op("sp", lambda e: e.dma_start(out=xsl[i][:, :], in_=src_ap), writes=["xsl%d" % i], akey="xsl%d" % i)
        return i

    xnq = [sb("xnq%d" % i, [128, D], BF16, T0 + 57344 + 2048 * i) for i in range(8)]

    def p1_norm(t, subs=(0, 1, 2, 3)):
        for s in subs:
            i = load_x(xb[(t * 4 + s) * 128:(t * 4 + s + 1) * 128, :])
            q = (t % 2) * 4 + s
            norm_only(xsl[i][:, :], "xsl%d" % i, gbc0, [cst], xnq[q], "xnq%d" % q, junk, "junk")

    def p1_front(t):
        slot = t % 2
        for s in range(4):
            q = (t % 2) * 4 + s
            tr_only(xnq[q], "xnq%d" % q, s, [0, 1, 2, 3])
        evac_transposes(hT[slot], "hT%d" % slot, [0, 1, 2, 3])

    def p1_back(t, between=None):
        slot = t % 2
        hk = "hT%d" % slot
        dsts = [(QTs, "QT_sb", 0.125), (KT_sb, "KT_sb", None), (QTd, "QT_d", 0.125), (KT_d, "KT_d", None)]
        for cg in range(4):
            if between is not None:
                between(cg)
            bi = 4 + cg % 2
            for kc in range(8):
                S.op("pe", lambda e, cg=cg, kc=kc, bi=bi: e.matmul(bank(bi), lhsT=wq[:, kc, cg * 128:(cg + 1) * 128], rhs=hT[slot][:, kc, :],
                                                              start=(kc == 0), stop=(kc == 7)),
                     reads=["wq", hk], writes=[pk(bi)])
            dst, dk, sc = dsts[cg]
            if sc is not None:
                for hh in range(2):
                    d_ap = dst[hh][64 * hh:64 * hh + 64, t * 512:(t + 1) * 512]
                    S.op("act", lambda e, d_ap=d_ap, bi=bi, sc=sc, hh=hh: e.activation(out=d_ap, in_=bank(bi, 0, 512, 64 * hh, 64 * hh + 64), func=AF.Copy, scale=sc),
                         reads=[pk(bi)] + QZ, writes=[(dk, t)])
            else:
                d_ap = dst[:, t * 512:(t + 1) * 512]
                S.op("dve", lambda e, d_ap=d_ap, bi=bi: e.tensor_copy(out=d_ap, in_=bank(bi)),
                     reads=[pk(bi)], writes=[(dk, t)])
        for s in range(4):
            bi = 6 + s % 2
            for kc in range(8):
                S.op("pe", lambda e, s=s, kc=kc, bi=bi: e.matmul(bank(bi, 0, 256), lhsT=hT[slot][:, kc, s * 128:(s + 1) * 128], rhs=wq[:, kc, 512:768],
                                                            start=(kc == 0), stop=(kc == 7)),
                     reads=["wq", hk], writes=[pk(bi)])
            kb = t * 4 + s
            if s % 2 == 0:
                S.op("dve", lambda e, kb=kb, bi=bi: e.tensor_copy(out=V_sb[:, kb, :], in_=bank(bi, 0, 128)), reads=[pk(bi)], writes=[("V_sb", kb)])
                S.op("dve", lambda e, kb=kb, bi=bi: e.tensor_copy(out=V_d[:, kb, :], in_=bank(bi, 128, 256)), reads=[pk(bi)], writes=[("V_d", kb)])
            else:
                S.op("act", lambda e, kb=kb, bi=bi: e.copy(out=V_sb[:, kb, :], in_=bank(bi, 0, 128)), reads=[pk(bi)], writes=[("V_sb", kb)])
                S.op("act", lambda e, kb=kb, bi=bi: e.copy(out=V_d[:, kb, :], in_=bank(bi, 128, 256)), reads=[pk(bi)], writes=[("V_d", kb)])

    for t in range(NQT + 1):
        if skip12:
            break
        if t == 0:
            p1_norm(0)
        if t < NQT:
            p1_front(t)
        nxt = (lambda cg, t=t: p1_norm(t + 1, (cg,))) if t + 1 < NQT else None
        if t >= 1:
            p1_back(t - 1, nxt)
        elif nxt is not None:
            for cg in range(4):
                nxt(cg)

    R_ATT = [(n, t) for n in ("QT_sb", "KT_sb", "QT_d", "KT_d") for t in range(NQT)] + \
            [(n, kb) for n in ("V_sb", "V_d") for kb in range(NKB)]

    if dbg:
        for i, (tn, nm) in enumerate(((QT_sb, "QT_sb"), (KT_sb, "KT_sb"), (QT_d, "QT_d"), (KT_d, "KT_d"))):
            S.op("sp", lambda e, i=i, tn=tn: e.dma_start(out=d_qk[i], in_=tn[:, :]), reads=[(nm, t) for t in range(NQT)], writes=[("dbg", i)], akey="dbg%d" % i)
        S.op("sp", lambda e: e.dma_start(out=d_v[0], in_=V_sb[:, :, :].rearrange("p a b -> p (a b)")), reads=[("V_sb", k) for k in range(NKB)], writes=[("dbg", 4)], akey="dbg4")
        S.op("sp", lambda e: e.dma_start(out=d_v[1], in_=V_d[:, :, :].rearrange("p a b -> p (a b)")), reads=[("V_d", k) for k in range(NKB)], writes=[("dbg", 5)], akey="dbg5")

    if stop_after >= 2 and not skip12:
        eB = [sb("eB%d" % i, [128, 512], F32, T0 + 2048 * i) for i in range(2)]
        LpB = [sb("LpB%d" % i, [128, 512], BF16, T0 + 4096 + 1024 * i) for i in range(2)]
        argB = [sb("argB%d" % i, [128, 512], F32, T0 + 6144 + 2048 * i) for i in range(2)]
        wB = [sb("wB%d" % i, [128, 512], BF16, T0 + 10240 + 1024 * i) for i in range(2)]
        carry = sb("carry", [128, 512], F32, T0 + 12288)
        stg = [[sb("stg%d_%d" % (s_, i), [128, 512], BF16, T0 + 14336 + 4096 * s_ + 1024 * i) for i in range(4)] for s_ in range(2)]
        aB = [[sb("aB%d_%d" % (s_, m), [128, 512], F32, T0 + 22528 + 4096 * s_ + 2048 * m) for m in range(2)] for s_ in range(2)]
        dwB = [[sb("dwB%d_%d" % (s_, m), [128, 512], BF16, T0 + 30720 + 2048 * s_ + 1024 * m) for m in range(2)] for s_ in range(2)]
        BT = sb("BT", [128, 5 * 512], F32, T0 + 34816)
        pvs = [sb("pvs%d" % m, [128, 512], F32, T0 + 45056 + 2048 * m) for m in range(2)]
        lsS = [sb("lsS%d" % m, [128, 512], F32, T0 + 49152 + 2048 * m) for m in range(2)]
        rB = [sb("rB%d" % m, [128, 512], F32, T0 + 53248 + 2048 * m) for m in range(2)]
        t1B = sb("t1B", [128, 512], F32, T0 + 57344)
        yvB = sb("yvB", [128, 512], F32, T0 + 59392)
        ysqB = sb("ysqB", [128, 512], BF16, T0 + 61440)
        lnvB = sb("lnvB", [128, 512], F32, T0 + 62464)
        rstB = sb("rstB", [128, 512], F32, T0 + 64512)
        ynB = sb("ynB", [128, 512], F32, T0 + 66560)
        dstg = [[sb("dstg%d_%d" % (s_, i), [128, 512], BF16, T0 + 68608 + 4096 * s_ + 1024 * i) for i in range(4)] for s_ in range(2)]
        P2_KEYS = (["eB0", "eB1", "LpB0", "LpB1", "argB0", "argB1", "wB0", "wB1", "carry", "BT",
                    "pvs0", "pvs1", "lsS0", "lsS1", "rB0", "rB1", "t1B", "yvB", "ysqB", "lnvB", "rstB", "ynB"]
                   + ["stg%d_%d" % (s_, i) for s_ in range(2) for i in range(4)]
                   + ["dstg%d_%d" % (s_, i) for s_ in range(2) for i in range(4)]
                   + ["aB%d_%d" % (s_, m) for s_ in range(2) for m in range(2)]
                   + ["dwB%d_%d" % (s_, m) for s_ in range(2) for m in range(2)])
        P2_KEYS = P2_KEYS + ["aP0", "aP1", "dwP0", "dwP1", ("wacc", 0), ("wacc", 1), "eP0", "eP1", "LpP0", "LpP1", "wP0", "wP1", "carryP", "sbev"] + ["stgP%d" % i for i in range(4)]
        S.alias(P2_KEYS, P1_KEYS)
        S.op("sp", lambda e: e.dma_start(out=BT[:, :], in_=btd[:, :]), writes=["BT"], akey="bt")

        rs_keys = []

        tiles = []

        eP = [sb("eP%d" % i, [128, 2, 512], F32, T0 + 4096 * i) for i in range(2)]
        LpP = [sb("LpP%d" % i, [128, 2, 512], BF16, T0 + 8192 + 2048 * i) for i in range(2)]
        wP = [sb("wP%d" % i, [128, 2, 512], BF16, T0 + 12288 + 2048 * i) for i in range(2)]
        stgP = [sb("stgP%d" % i, [128, 512], BF16, T0 + 16384 + 1024 * i) for i in range(4)]
        carryP = sb("carryP", [128, 2, 512], F32, T0 + 53248)
        sbev = sb("sbev", [128, 512], F32, T0 + 20480)

        def psP(b0, cs):
            return ps[:, b0 * 512:(b0 + 2) * 512].rearrange("p (m n) -> p m n", m=2)[:, :, cs:512]

        def sb_z(i):
            T = tiles[i]
            s_ = i % 2
            cs = T["cs"]
            q0 = T["qt"] * 512
            k0 = T["kb"] * 128
            diag = T["j"] >= 0
            for h in range(2):
                S.op("pe", lambda e, h=h: e.matmul(bank(h, cs, 512), lhsT=KT_sb[:, k0:k0 + 128], rhs=QTs[h][:, q0 + cs:q0 + 512],
                                                   start=True, stop=not diag),
                     reads=[("KT_sb", T["kb"] // 4), ("QT_sb", T["qt"])], writes=[pk(h)])
                if diag:
                    S.op("pe", lambda e, h=h: e.matmul(bank(h, cs, cs + 128), lhsT=ident[:, :], rhs=mtri[:, :], start=False, stop=True),
                         reads=["ident", "mtri"], writes=[pk(h)])
            S.op("act", lambda e: e.activation(out=eP[s_][:, :, cs:512], in_=psP(0, cs), func=AF.Exp),
                 reads=[pk(0), pk(1)], writes=["eP%d" % s_])
            S.op("act", lambda e: e.activation(out=LpP[s_][:, :, cs:512], in_=eP[s_][:, :, cs:512], func=AF.Ln, bias=1.0, scale=1.0),
                 reads=["eP%d" % s_], writes=["LpP%d" % s_])

        def sb_mid(i):
            T = tiles[i]
            s_ = i % 2
            cs = T["cs"]
            q0 = T["qt"] * 512
            k0 = T["kb"] * 128
            diag = T["j"] >= 0
            for h in range(2):
                bB = 2 + h
                S.op("pe", lambda e, h=h, bB=bB: e.matmul(bank(bB, cs, 512), lhsT=uneg[:, :], rhs=LpP[s_][:, h, cs:512], start=True, stop=False),
                     reads=["uneg", "LpP%d" % s_], writes=[pk(bB)])
                S.op("pe", lambda e, h=h, bB=bB: e.matmul(bank(bB, cs, 512), lhsT=KT_sb[:, k0:k0 + 128], rhs=QTs[h][:, q0 + cs:q0 + 512],
                                                          start=False, stop=not diag),
                     reads=[("KT_sb", T["kb"] // 4), ("QT_sb", T["qt"])], writes=[pk(bB)])
                if diag:
                    S.op("pe", lambda e, bB=bB: e.matmul(bank(bB, cs, cs + 128), lhsT=ident[:, :], rhs=mtri[:, :], start=False, stop=True),
                         reads=["ident", "mtri"], writes=[pk(bB)])
            if not T["last"]:
                for h in range(2):
                    bC = 4 + h
                    S.op("pe", lambda e, h=h, bC=bC: e.matmul(bank(bC, cs, 512), lhsT=ones[:, :], rhs=LpP[s_][:, h, cs:512], start=True, stop=True),
                         reads=["ones", "LpP%d" % s_], writes=[pk(bC)])
            if T["first"]:
                S.op("dve", lambda e: e.memset(carryP[:, :, :], 0.0), writes=["carryP"])
            S.op("dve", lambda e: e.tensor_tensor(out=eP[s_][:, :, cs:512], in0=psP(2, cs), in1=carryP[:, :, cs:512], op=ALU.subtract),
                 reads=[pk(2), pk(3), "carryP"], writes=["eP%d" % s_])
            if not T["last"]:
                S.op("dve", lambda e: e.tensor_tensor(out=carryP[:, :, cs:512], in0=psP(4, cs), in1=carryP[:, :, cs:512], op=ALU.add),
                     reads=[pk(4), pk(5), "carryP"], writes=["carryP"])
            S.op("act", lambda e: e.activation(out=wP[s_][:, :, cs:512], in_=eP[s_][:, :, cs:512], func=AF.Exp),
                 reads=["eP%d" % s_], writes=["wP%d" % s_])

        def sb_pv(i):
            T = tiles[i]
            s_ = i % 2
            cs = T["cs"]
            kb = T["kb"]
            qt = T["qt"]
            for h in range(2):
                S.op("pe", lambda e, h=h: e.matmul(bank(6 + h, cs, 512), lhsT=V_sb[:, kb, :], rhs=wP[s_][:, h, cs:512],
                                                   start=T["first"], stop=T["last"], skip_group_check=True),
                     reads=[("V_sb", kb), "wP%d" % s_], writes=[pk(6 + h)])
            if T["last"]:
                for h in range(2):
                    S.op("dve", lambda e, h=h: e.tensor_copy(out=sbev[64 * h:64 * h + 64, :], in_=bank(6 + h, 0, 512, 64 * h, 64 * h + 64)),
                         reads=[pk(6 + h)] + (["sbev"] if h == 1 else []), writes=["sbev"])
                for i4 in range(4):
                    st = stgP[i4]
                    skey = "stgP%d" % i4
                    S.op("pool", lambda e, st=st, i4=i4: e.tensor_scalar(out=st[:, :], in0=sbev[:, :], scalar1=onehot[:, i4:i4 + 1], scalar2=None, op0=ALU.mult),
                         reads=["sbev", cst], writes=[skey])
                    r0 = (qt // 4) * 1024 + 128 * i4
                    c0 = (qt % 4) * 512 if cur["gi"] == 0 else 0
                    rin = rs_in_l[cur["gi"]]
                    rk = ("rs_in", 0, qt, i4)
                    rs_keys.append(rk)
                    S.op("sp", lambda e, st=st, r0=r0, c0=c0, rin=rin: e.dma_start(out=rin[r0:r0 + 128, c0:c0 + 512], in_=st[:, :]),
                         reads=[skey], writes=[rk], akey=skey)

        dt_tiles = []
        dcnt = [0]

        waccT = sb("waccT", [128, 2, 512], F32, T0 + 53248)
        aP = [sb("aP%d" % s_, [128, 2, 512], F32, T0 + 22528 + 4096 * s_) for s_ in range(2)]
        dwP = [sb("dwP%d" % s_, [128, 2, 512], BF16, T0 + 30720 + 2048 * s_) for s_ in range(2)]

        def pair_ps(s_, cs):
            return ps[:, (2 * s_) * 512:(2 * s_ + 2) * 512].rearrange("p (m n) -> p m n", m=2)[:, :, cs:512]

        def df_z(i):
            T = dt_tiles[i]
            s_ = i % 2
            cs = T["cs"]
            q0 = T["qt"] * 512
            k0 = T["kb"] * 128
            wk = "dwP%d" % s_
            for m in range(2):
                bz = 2 * s_ + m
                S.op("pe", lambda e, m=m, bz=bz: e.matmul(bank(bz, cs, 512), lhsT=KT_d[:, k0:k0 + 128],
                                                          rhs=QTd[m][:, q0 + cs:q0 + 512], start=True, stop=True),
                     reads=[("KT_d", T["kb"] // 4), ("QT_d", T["qt"])], writes=[pk(bz)])
            if T["near"]:
                jj = T["j"] + 1
                ak = "aP%d" % s_
                for m in range(2):
                    bz = 2 * s_ + m
                    S.op("dve", lambda e, m=m, bz=bz, jj=jj: e.tensor_tensor(out=aP[s_][:, m, cs:512], in0=bank(bz, cs, 512),
                                                                         in1=BT[:, jj * 512 + cs:(jj + 1) * 512], op=ALU.add),
                         reads=[pk(bz), "BT"] + ([ak] if m == 1 else []), writes=[ak])
                S.op("act", lambda e: e.activation(out=dwP[s_][:, :, cs:512], in_=aP[s_][:, :, cs:512], func=AF.Exp),
                     reads=[ak], writes=[wk])
            else:
                S.op("act", lambda e: e.activation(out=dwP[s_][:, :, :], in_=pair_ps(s_, 0), func=AF.Exp, bias=c15[:, 0:1], scale=1.0),
                     reads=[pk(2 * s_), pk(2 * s_ + 1), cst], writes=[wk])

        dpend = []

        def df_flush(force):
            for ent in list(dpend):
                ent[0] -= 1
                if force or ent[0] <= 0:
                    dpend.remove(ent)
                    ent[1]()

        def df_pv(i):
            T = dt_tiles[i]
            s_ = i % 2
            cs = T["cs"]
            kb = T["kb"]
            qt = T["qt"]
            wk = "dwP%d" % s_
            df_flush(T["last"])
            for m in range(2):
                S.op("pe", lambda e, m=m: e.matmul(bank(4 + m, cs, 512), lhsT=V_d[:, kb, :], rhs=dwP[s_][:, m, cs:512],
                                                   start=T["first"], stop=T["last"], skip_group_check=True),
                     reads=[("V_d", kb), wk], writes=[pk(4 + m)])
                S.op("pe", lambda e, m=m: e.matmul(bank(6 + m, cs, 512), lhsT=ones[:, :], rhs=dwP[s_][:, m, cs:512],
                                                   start=T["first"], stop=T["last"], skip_group_check=True),
                     reads=["ones", wk], writes=[pk(6 + m)])
            if T["last"]:
                S.op("dve", lambda e: e.tensor_copy(out=pvs[0][:, :], in_=bank(4)), reads=[pk(4)], writes=["pvs0"])
                S.op("dve", lambda e: e.tensor_copy(out=pvs[1][:, :], in_=bank(5)), reads=[pk(5)], writes=["pvs1"])
                S.op("act", lambda e: e.copy(out=lsS[0][:, :], in_=bank(6)), reads=[pk(6)], writes=["lsS0"])
                S.op("act", lambda e: e.copy(out=lsS[1][:, :], in_=bank(7)), reads=[pk(7)], writes=["lsS1"])
                for m in range(2):
                    S.op("dve", lambda e, m=m: e.reciprocal(out=lsS[m][:, :], in_=lsS[m][:, :]), reads=["lsS%d" % m], writes=["lsS%d" % m])
                S.op("dve", lambda e: e.tensor_tensor(out=t1B[:, :], in0=pvs[0][:, :], in1=lsS[0][:, :], op=ALU.mult), reads=["pvs0", "lsS0"], writes=["t1B"])
                S.op("dve", lambda e: e.tensor_tensor(out=pvs[1][:, :], in0=pvs[1][:, :], in1=lsS[1][:, :], op=ALU.mult), reads=["pvs1", "lsS1"], writes=["pvs1"])
                S.op("dve", lambda e: e.scalar_tensor_tensor(out=yvB[:, :], in0=pvs[1][:, :], scalar=neglam, in1=t1B[:, :], op0=ALU.mult, op1=ALU.add),
                     reads=["pvs1", "t1B", "lamS"], writes=["yvB"])
                S.op("pool", lambda e: e.tensor_tensor(out=ysqB[:, :], in0=yvB[:, :], in1=yvB[:, :], op=ALU.mult), reads=["yvB"], writes=["ysqB"])
                gi_now = cur["gi"]
                dpend.append([3, lambda qt=qt, gi_now=gi_now: df_tail(qt, gi_now)])

        def df_tail(qt, gi_now):
                S.op("pe", lambda e: e.matmul(bank(0), lhsT=onesm[:, :], rhs=ysqB[:, :], start=True, stop=True), reads=["onesm", "ysqB"], writes=[pk(0)])
                S.op("act", lambda e: e.activation(out=lnvB[:, :], in_=bank(0), func=AF.Ln, bias=EPS, scale=1.0), reads=[pk(0)], writes=["lnvB"])
                S.op("act", lambda e: e.activation(out=rstB[:, :], in_=lnvB[:, :], func=AF.Exp, scale=-0.5), reads=["lnvB"], writes=["rstB"])
                S.op("dve", lambda e: e.scalar_tensor_tensor(out=ynB[:, :], in0=yvB[:, :], scalar=sublnS[:, 0:1], in1=rstB[:, :], op0=ALU.mult, op1=ALU.mult),
                     reads=["yvB", "rstB", "sublnS"], writes=["ynB"])
                ss_ = dcnt[0] % 2
                dcnt[0] += 1
                for i4 in range(4):
                    st = dstg[ss_][i4]
                    skey = "dstg%d_%d" % (ss_, i4)
                    S.op("pool", lambda e, st=st, i4=i4: e.tensor_scalar(out=st[:, :], in0=ynB[:, :], scalar1=onehot[:, i4:i4 + 1], scalar2=None, op0=ALU.mult),
                         reads=["ynB", cst], writes=[skey])
                    r0 = (qt // 4) * 1024 + 512 + 128 * i4
                    c0 = (qt % 4) * 512 if gi_now == 0 else 0
                    rin = rs_in_l[gi_now]
                    rk = ("rs_in", 2, qt, i4)
                    rs_keys.append(rk)
                    S.op("sp", lambda e, st=st, r0=r0, c0=c0, rin=rin: e.dma_start(out=rin[r0:r0 + 128, c0:c0 + 512], in_=st[:, :]),
                         reads=[skey], writes=[rk], akey=skey)

        Wg = sb("Wg", [128, 8, 2 * D], BF16, R0 + 0)
        Wo_ = sb("Wo", [128, 8, D], BF16, R0 + 65536)
        for gi, qts in enumerate(GROUPS):
            cur["gi"] = gi
            rs_keys = []
            tiles = []
            for qt in qts:
                for kb in range(4 * qt + 3, -1, -1):
                    j = kb - 4 * qt
                    tiles.append(dict(qt=qt, kb=kb, j=j, cs=max(j, 0) * 128, first=(kb == 4 * qt + 3), last=(kb == 0)))
            N = len(tiles)
            for t in range(N + 2):
                if t < N:
                    sb_z(t)
                if 1 <= t <= N:
                    sb_mid(t - 1)
                if t >= 2:
                    sb_pv(t - 2)
            if gi == len(GROUPS) - 1 and stop_after >= 3:
                S.alias(["Wg"], [(n, t) for n in ("QT_sb", "KT_sb") for t in range(NQT)])
                for kc in range(8):
                    S.op("pool", lambda e, kc=kc: e.dma_start(out=Wg[:, kc, :], in_=wgate[kc * 128:(kc + 1) * 128, :]), writes=["Wg"], akey="Wg", chain=True)
                S.alias(["Wo"], [("V_sb", kb) for kb in range(NKB)])
                for kc in range(8):
                    S.op("pool", lambda e, kc=kc: e.dma_start(out=Wo_[:, kc, :], in_=wo[kc * 128:(kc + 1) * 128, :]), writes=["Wo"], akey="Wo", chain=True)
            dt_tiles = []
            for qt in qts:
                for kb in range(0, 4 * qt + 4):
                    j = kb - 4 * qt
                    dt_tiles.append(dict(qt=qt, kb=kb, j=j, cs=max(j, 0) * 128, first=(kb == 0), last=(kb == 4 * qt + 3), near=(j >= -1)))
            ND = len(dt_tiles)
            for t in range(ND + 1):
                if t < ND:
                    df_z(t)
                if t >= 1:
                    df_pv(t - 1)
            df_flush(True)
            S.op("pool", lambda e, gi=gi: e.collective_compute("ReduceScatter", ALU.add, replica_groups=[[0, 1, 2, 3], [4, 5, 6, 7]],
                                                               ins=[rs_in_l[gi].ap().opt()], outs=[rs_out_l[gi].ap().opt()]),
                 reads=rs_keys, writes=["rs_out%d" % gi], akey="cc%d" % gi, inc=1)
        if dbg:
            S.op("pool", lambda e: e.dma_start(out=d_rs[:, 0:1536], in_=rs_out_l[0][:, :]), reads=["rs_out0"], writes=[("dbg", 6)], akey="dbg6")
            S.op("pool", lambda e: e.dma_start(out=d_rs[:, 1536:2048], in_=rs_out_l[1][:, :]), reads=["rs_out1"], writes=[("dbg", 8)], akey="dbg8")

    if stop_after >= 3:
        Wsb_ = sb("Wsb", [128, 4, D], BF16, R0 + 81920)
        Wdf_ = sb("Wdf", [128, 4, D], BF16, R0 + 90112)
        yT = sb("yT", [128, 8, 512], BF16, R0 + 32768)
        mT = sb("mT", [128, 8, 512], BF16, R0 + 40960)
        gate = [[sb("gate%d_%d" % (p_, i), [128, 512], F32, (R0 + 49152 if p_ == 0 else T0 + 100352) + 2048 * i) for i in range(4)] for p_ in range(2)]
        gbcA0 = sb("gbcA0", [128, D], F32, R0 + 57344)
        gbcA1 = sb("gbcA1", [128, D], F32, R0 + 61440)
        x1all = sb("x1all", [128, 16, D], F32, T0 + 0)
        xsl3 = [sb("xsl3_%d" % i, [128, D], F32, T0 + 65536 + 4096 * i) for i in range(4)]
        junk3 = sb("junk3", [128, D], BF16, T0 + 81920)
        xn3 = [sb("xn3_%d" % i, [128, D], BF16, T0 + 83968 + 2048 * i) for i in range(2)]
        hT3 = sb("hT3", [128, 8, 512], BF16, T0 + 88064)
        tmp3 = sb("tmp3", [128, D], F32, T0 + 96256)
        nb3 = (junk3, "junk3", xn3, ["xn3_0", "xn3_1"])
        SA_R = ["Wg", "Wsb", "Wdf", "Wo", "yT", "mT", "gbcA"] + ["gate0_%d" % i for i in range(4)]
        S.alias([k for k in SA_R if k not in ("Wg", "Wo")], R_ATT)
        if skip12:
            Wg = sb("Wg", [128, 8, 2 * D], BF16, R0 + 0)
            Wo_ = sb("Wo", [128, 8, D], BF16, R0 + 65536)
            for kc in range(8):
                S.op("pool", lambda e, kc=kc: e.dma_start(out=Wg[:, kc, :], in_=wgate[kc * 128:(kc + 1) * 128, :]), writes=["Wg"], akey="Wg", chain=True)
            for kc in range(8):
                S.op("pool", lambda e, kc=kc: e.dma_start(out=Wo_[:, kc, :], in_=wo[kc * 128:(kc + 1) * 128, :]), writes=["Wo"], akey="Wo", chain=True)
        P3_T = ([("x1", i) for i in range(16)] + ["xsl3_%d" % i for i in range(4)] + ["junk3", "xn3_0", "xn3_1", "hT3", "tmp3"]
                + ["gate1_%d" % i for i in range(4)])
        S.alias(P3_T, (P2_KEYS + R_ATT) if not skip12 else [])

        for kc in range(4):
            S.op("pool", lambda e, kc=kc: e.dma_start(out=Wsb_[:, kc, :], in_=wsb[kc * 128:(kc + 1) * 128, :]), writes=["Wsb"], akey="Wsb", chain=True)
        for kc in range(4):
            S.op("pool", lambda e, kc=kc: e.dma_start(out=Wdf_[:, kc, :], in_=wdf[kc * 128:(kc + 1) * 128, :]), writes=["Wdf"], akey="Wdf", chain=True)
        S.op("sp", lambda e: e.dma_start(out=gbcA0[:, :], in_=gvec[0:1, :].partition_broadcast(128)), writes=["gbcA"], akey="gbcA", chain=True)
        S.op("sp", lambda e: e.dma_start(out=gbcA1[:, :], in_=gvec[1:2, :].partition_broadcast(128)), writes=["gbcA"], akey="gbcA", chain=True)

        x3cnt = [0]

        def load_x3(src_ap):
            i = x3cnt[0] % 4
            x3cnt[0] += 1
            S.op("sp", lambda e: e.dma_start(out=xsl3[i][:, :], in_=src_ap), writes=["xsl3_%d" % i], akey="xsl%d" % i)
            return i

        def post_norm_residual(p_, gbc, gkey, res_ap, res_key, dst_ap, dst_key):
            c = (cnt["stat"] % 8) * 4
            cnt["stat"] += 1
            skey = ("stat", c)
            zap = ps[:, 2 * p_ * 512:(2 * p_ + 2) * 512]
            S.op("act", lambda e: e.activation(out=junk3[:, :], in_=zap, func=AF.Square, accum_out=stat[:, c:c + 1]),
                 reads=[pk(2 * p_), pk(2 * p_ + 1)], writes=["junk3", skey], noinline=True)
            S.op("act", lambda e: e.activation(out=stat[:, c + 1:c + 2], in_=stat[:, c:c + 1], func=AF.Ln, scale=1.0 / D, bias=EPS),
                 reads=[skey], writes=[skey])
            S.op("act", lambda e: e.activation(out=stat[:, c + 2:c + 3], in_=stat[:, c + 1:c + 2], func=AF.Exp, scale=-0.5),
                 reads=[skey], writes=[skey])
            S.op("dve", lambda e: e.scalar_tensor_tensor(out=tmp3[:, :], in0=zap, scalar=stat[:, c + 2:c + 3], in1=gbc[:, :], op0=ALU.mult, op1=ALU.mult),
                 reads=[pk(2 * p_), pk(2 * p_ + 1), skey, gkey], writes=["tmp3"])
            S.op("dve", lambda e: e.tensor_tensor(out=dst_ap, in0=tmp3[:, :], in1=res_ap, op=ALU.add),
                 reads=["tmp3", res_key], writes=[dst_key])

        for tt in range(4):
            rsi, rc0 = (0, tt * 512) if tt < 3 else (1, 0)
            S.op("sp", lambda e, rsi=rsi, rc0=rc0: e.dma_start(out=yT[:, :, :], in_=rs_out_l[rsi].ap().rearrange("(kc p) n -> p kc n", p=128)[:, :, rc0:rc0 + 512]),
                 reads=["rs_out%d" % rsi], writes=["yT"], akey="yT")
            xslots = []
            for s in range(4):
                i = load_x3(xs[(tt * 4 + s) * 128:(tt * 4 + s + 1) * 128, :])
                xslots.append(i)
                norm_transpose(xsl3[i][:, :], "xsl3_%d" % i, gbcA0, ["gbcA"], hT3, "hT3", s, [0, 1, 2, 3], nb3)
            evac_transposes(hT3, "hT3", [0, 1, 2, 3])
            if p3_limit <= 1:
                break
            for oc in range(8):
                p_ = oc % 2
                b4 = [4 * p_ + i for i in range(4)]
                gk = ["gate%d_%d" % (p_, i) for i in range(4)]
                gb = gate[p_]
                for kc in range(8):
                    S.op("pe", lambda e, kc=kc, oc=oc, b=b4[2]: e.matmul(bank(b), lhsT=Wg[:, kc, oc * 128:(oc + 1) * 128], rhs=hT3[:, kc, :], start=(kc == 0), stop=(kc == 7)),
                         reads=["Wg", "hT3"], writes=[pk(b4[2])])
                for kc in range(8):
                    S.op("pe", lambda e, kc=kc, oc=oc, b=b4[3]: e.matmul(bank(b), lhsT=Wg[:, kc, D + oc * 128:D + (oc + 1) * 128], rhs=hT3[:, kc, :], start=(kc == 0), stop=(kc == 7)),
                         reads=["Wg", "hT3"], writes=[pk(b4[3])])
                for kc in range(4):
                    S.op("pe", lambda e, kc=kc, oc=oc, b=b4[0]: e.matmul(bank(b), lhsT=Wsb_[:, kc, oc * 128:(oc + 1) * 128], rhs=yT[:, kc, :], start=(kc == 0), stop=(kc == 3)),
                         reads=["Wsb", "yT"], writes=[pk(b4[0])])
                for kc in range(4):
                    S.op("pe", lambda e, kc=kc, oc=oc, b=b4[1]: e.matmul(bank(b), lhsT=Wdf_[:, kc, oc * 128:(oc + 1) * 128], rhs=yT[:, 4 + kc, :], start=(kc == 0), stop=(kc == 3)),
                         reads=["Wdf", "yT"], writes=[pk(b4[1])])
                S.op("act", lambda e, gb=gb, b=b4[2]: e.activation(out=gb[0][:, :], in_=bank(b), func=AF.Sigmoid), reads=[pk(b4[2])], writes=[gk[0]])
                S.op("act", lambda e, gb=gb, b=b4[3]: e.activation(out=gb[1][:, :], in_=bank(b), func=AF.Sigmoid), reads=[pk(b4[3])], writes=[gk[1]])
                S.op("dve", lambda e, gb=gb, b=b4[0]: e.tensor_tensor(out=gb[2][:, :], in0=bank(b), in1=gb[0][:, :], op=ALU.mult), reads=[pk(b4[0]), gk[0]], writes=[gk[2]])
                S.op("dve", lambda e, gb=gb, b=b4[1]: e.tensor_tensor(out=gb[3][:, :], in0=bank(b), in1=gb[1][:, :], op=ALU.mult), reads=[pk(b4[1]), gk[1]], writes=[gk[3]])
                S.op("dve", lambda e, gb=gb, oc=oc: e.tensor_tensor(out=mT[:, oc, :], in0=gb[2][:, :], in1=gb[3][:, :], op=ALU.add), reads=[gk[2], gk[3]], writes=["mT"])
            if p3_limit <= 2:
                break
            for s in range(4):
                for nh in range(2):
                    b = 2 * s + nh
                    for kc in range(8):
                        S.op("pe", lambda e, kc=kc, s=s, nh=nh, b=b: e.matmul(bank(b), lhsT=mT[:, kc, s * 128:(s + 1) * 128], rhs=Wo_[:, kc, nh * 512:(nh + 1) * 512],
                                                                          start=(kc == 0), stop=(kc == 7)),
                             reads=["mT", "Wo"], writes=[pk(b)])
                i = xslots[s]
                post_norm_residual(s, gbcA1, "gbcA", xsl3[i][:, :], "xsl3_%d" % i, x1all[:, tt * 4 + s, :], ("x1", tt * 4 + s))
            if p3_limit <= 3:
                break

        if dbg:
            d_x1 = nc.dram_tensor("d_x1", [OWN, D], F32, kind="ExternalOutput").ap()
            S.op("sp", lambda e: e.dma_start(out=d_x1.rearrange("(a p) d -> p a d", p=128), in_=x1all[:, :, :]), reads=[("x1", i) for i in range(16)],
                 writes=[("dbg", 7)], akey="dbg7")

        WdnS = [sb("WdnS%d" % i, [128, 4, D], BF16, R0 + 8192 * i) for i in range(4)]
        uT = sb("uT", [128, 32, 512], BF16, R0 + 32768)
        WupS = [sb("WupS%d" % i, [128, 8, 512], BF16, R0 + 65536 + 8192 * i) for i in range(4)]
        gbcB0 = sb("gbcB0", [128, D], F32, T0 + 65536)
        gbcB1 = sb("gbcB1", [128, D], F32, T0 + 69632)
        rl = [sb("rl%d" % i, [128, 512], F32, T0 + 73728 + 2048 * i) for i in range(2)]
        SB_R = ["WdnS%d" % i for i in range(4)] + ["WupS%d" % i for i in range(4)] + [("uT", i) for i in range(32)]
        S.alias(SB_R, SA_R)
        S.alias(["gbcB", "rl0", "rl1"], ["xsl3_%d" % i for i in range(4)])
        S.op("sp", lambda e: e.dma_start(out=gbcB0[:, :], in_=gvec[2:3, :].partition_broadcast(128)), writes=["gbcB"], akey="gbcB", chain=True)
        S.op("sp", lambda e: e.dma_start(out=gbcB1[:, :], in_=gvec[3:4, :].partition_broadcast(128)), writes=["gbcB"], akey="gbcB", chain=True)

        seq = []
        for tt in range(4):
            seq += [("up", tt, g_) for g_ in range(8)] + [("dn", tt, g_) for g_ in range(8)]
        wptr = [0]
        slot_of = {}
        cnts = {"up": 0, "dn": 0}
        wup_v = wup.rearrange("(kc p) n -> p kc n", p=128)
        wdn_v = wdn.rearrange("(fc p) n -> p fc n", p=128)

        def ensure_loaded(n):
            while wptr[0] <= min(n, len(seq) - 1):
                kind, tt_, g_ = seq[wptr[0]]
                sl = cnts[kind] % 4
                cnts[kind] += 1
                slot_of[seq[wptr[0]]] = sl
                if kind == "up":
                    S.op("pool", lambda e, sl=sl, g_=g_: e.dma_start(out=WupS[sl][:, :, :], in_=wup_v[:, :, g_ * 512:(g_ + 1) * 512]),
                         writes=["WupS%d" % sl], akey="wup%d" % sl)
                else:
                    S.op("pool", lambda e, sl=sl, g_=g_: e.dma_start(out=WdnS[sl][:, :, :], in_=wdn_v[:, g_ * 4:(g_ + 1) * 4, :]),
                         writes=["WdnS%d" % sl], akey="wdn%d" % sl)
                wptr[0] += 1

        LOOK = 3
        rlc = [0]
        hT3b = sb("hT3b", [128, 8, 512], BF16, T0 + 100352)
        S.alias(["hT3b"], ["gate1_%d" % i for i in range(4)])
        hTB = [(hT3, "hT3"), (hT3b, "hT3b")]

        def mlp_front(tt_):
            ht, hk = hTB[tt_ % 2]
            for s in range(4):
                norm_transpose(x1all[:, tt_ * 4 + s, :], ("x1", tt_ * 4 + s), gbcB0, ["gbcB"], ht, hk, s, [0, 1, 2, 3], nb3)
            evac_transposes(ht, hk, [0, 1, 2, 3])

        for tt in range(4 if p3_limit >= 5 else 0):
            ensure_loaded(tt * 16 + LOOK)
            if tt == 0:
                mlp_front(0)
            hT3c, hT3k = hTB[tt % 2]
            for g_ in range(8):
                if g_ == 2 and tt + 1 < 4 and p3_limit >= 99:
                    mlp_front(tt + 1)
                n = tt * 16 + g_
                ensure_loaded(n + LOOK)
                sl = slot_of[("up", tt, g_)]
                for fcl in range(4):
                    fc = g_ * 4 + fcl
                    b = 4 + fc % 4
                    for kc in range(8):
                        S.op("pe", lambda e, kc=kc, fcl=fcl, b=b, sl=sl, hT3c=hT3c: e.matmul(bank(b), lhsT=WupS[sl][:, kc, fcl * 128:(fcl + 1) * 128], rhs=hT3c[:, kc, :],
                                                                              start=(kc == 0), stop=(kc == 7)),
                             reads=["WupS%d" % sl, hT3k], writes=[pk(b)])
                    ri = rlc[0] % 2
                    rlc[0] += 1
                    S.op("act", lambda e, b=b, ri=ri: e.activation(out=rl[ri][:, :], in_=bank(b), func=AF.Relu), reads=[pk(b)], writes=["rl%d" % ri])
                    S.op("dve", lambda e, fc=fc, ri=ri: e.tensor_tensor(out=uT[:, fc, :], in0=rl[ri][:, :], in1=rl[ri][:, :], op=ALU.mult),
                         reads=["rl%d" % ri], writes=[("uT", fc)])
            if p3_limit <= 5:
                break
            for g_ in range(8):
                n = tt * 16 + 8 + g_
                ensure_loaded(n + LOOK)
                sl = slot_of[("dn", tt, g_)]
                for s in range(4):
                    for nh in range(2):
                        b = 2 * s + nh
                        for fcl in range(4):
                            fc = g_ * 4 + fcl
                            S.op("pe", lambda e, fc=fc, fcl=fcl, s=s, nh=nh, b=b, sl=sl, g_=g_: e.matmul(
                                bank(b), lhsT=uT[:, fc, s * 128:(s + 1) * 128], rhs=WdnS[sl][:, fcl, nh * 512:(nh + 1) * 512],
                                start=(g_ == 0 and fcl == 0), stop=(g_ == 7 and fcl == 3)),
                                 reads=[("uT", fc), "WdnS%d" % sl], writes=[pk(b)])
            for s in (2, 3, 0, 1):
                idx = tt * 4 + s
                post_norm_residual(s, gbcB1, "gbcB", x1all[:, idx, :], ("x1", idx), x1all[:, idx, :], ("x1", idx))
                S.op("sp", lambda e, idx=idx: e.dma_start(out=out[idx * 128:(idx + 1) * 128, :], in_=x1all[:, idx, :]),
                     reads=[("x1", idx)], writes=[("out", idx)], akey="o%d" % (idx % 4))
            if p3_limit <= 6:
                break


    if stop_after < 3:
        S.op("sp", lambda e: e.dma_start(out=out[0:128, :], in_=xsl[0][:, :]), reads=["xsl0"], writes=["outdummy"], akey="outd")
    S.emit()
    return nc


def t5_bucket_np(rel):
    half = 16
    max_exact = 8
    ret = np.where(rel > 0, half, 0)
    n = np.abs(rel)
    nf = np.maximum(n, 1).astype(np.float32)
    large = max_exact + (np.log(nf / max_exact) / math.log(128 / max_exact) * (half - max_exact)).astype(np.int32)
    large = np.minimum(large, half - 1)
    return ret + np.where(n < max_exact, n, large)


def make_bias_tiles(rel_bias, g):
    kk = np.arange(128)[:, None]
    qq = np.arange(128)[None, :]
    bt = np.empty((128, 5, 4, 128), np.float32)
    for jj in range(5):
        j = jj - 1
        for r in range(4):
            delta = (j - r) * 128
            rel = delta + kk - qq
            vals = rel_bias[t5_bucket_np(rel), g]
            if j > r:
                allowed = np.zeros((128, 128), bool)
            elif j == r:
                allowed = kk < (qq // 64 + 1) * 64
            else:
                allowed = np.ones((128, 128), bool)
            bt[:, jj, r, :] = np.where(allowed, vals, np.float32(NEG))
    return np.ascontiguousarray(bt.reshape(128, 5 * 512))


def make_in_maps(inputs):
    x = np.asarray(inputs["x"], np.float32)
    w_in = np.asarray(inputs["w_in"], np.float32)[0]
    rel_bias = np.asarray(inputs["rel_bias"], np.float32)
    gvec = np.stack([np.asarray(inputs[k], np.float32)[0] for k in ("g_pre_mix", "g_post_mix", "g_pre_mlp", "g_post_mlp")])
    lamv = np.concatenate([np.asarray(inputs[k], np.float32)[0] for k in ("lambda_q1", "lambda_k1", "lambda_q2", "lambda_k2")])[None, :]
    subln = np.ascontiguousarray(np.asarray(inputs["w_subln"], np.float32)[0][:, None])
    shared = dict(
        wgate=np.ascontiguousarray(w_in[:, 3072:5120]),
        wsb=np.asarray(inputs["w_sb_out"], np.float32)[0],
        wdf=np.asarray(inputs["w_diff_out"], np.float32)[0],
        wo=np.asarray(inputs["w_o"], np.float32)[0],
        wup=np.asarray(inputs["w_up"], np.float32)[0],
        wdn=np.asarray(inputs["w_down"], np.float32)[0],
        gvec=np.ascontiguousarray(gvec), lamv=np.ascontiguousarray(lamv), subln=subln,
    )
    maps = []
    for c in range(8):
        b, g = c // 4, c % 4
        cols = np.concatenate([np.arange(0 + 128 * g, 0 + 128 * g + 128), np.arange(512 + 128 * g, 512 + 128 * g + 128),
                               np.arange(1536 + 128 * g, 1536 + 128 * g + 128), np.arange(2048 + 128 * g, 2048 + 128 * g + 128),
                               np.arange(1024 + 128 * g, 1024 + 128 * g + 128), np.arange(2560 + 128 * g, 2560 + 128 * g + 128)])
        oh = np.zeros((128, 4), np.float32)
        oh[:, g] = 1.0
        m = dict(shared)
        m.update(
            xb=np.ascontiguousarray(x[b]),
            xs=np.ascontiguousarray(x[b, g * OWN:(g + 1) * OWN]),
            wqkv=np.ascontiguousarray(w_in[:, cols]),
            bt=make_bias_tiles(rel_bias, g),
            c15=np.full((128, 1), rel_bias[15, g], np.float32),
            onehot=oh,
        )
        maps.append(m)
    return maps


_NC_CACHE = {}


def kernel(**inputs):
    if "nc" not in _NC_CACHE:
        _NC_CACHE["nc"] = build()
    nc = _NC_CACHE["nc"]
    maps = make_in_maps(inputs)
    res = run_bass_kernel_spmd(nc, maps, core_ids=list(range(8)))
    outp = np.empty((2, SEQ, D), np.float32)
    for c in range(8):
        b, g = c // 4, c % 4
        outp[b, g * OWN:(g + 1) * OWN] = res.results[c]["out"]
    return outp
```

```python
import math
import numpy as np
import ml_dtypes
import concourse.bass as bass
import concourse.mybir as mybir
from concourse.bass_utils import run_bass_kernel_spmd

F32 = mybir.dt.float32
BF16 = mybir.dt.bfloat16
AF = mybir.ActivationFunctionType
ALU = mybir.AluOpType
AX = mybir.AxisListType

SEQ = 8192
D = 1024
DFF = 4096
NQT = 16
NKB = 64
OWN = 2048
EPS = 1e-6
NEG = -30000.0
LAM_INIT = 0.8 - 0.6 * math.exp(-0.3 * 0)
ENGS = ("pe", "act", "dve", "pool", "sp")


class Op:
    __slots__ = ("eng", "fn", "deps", "is_async", "inc", "sem", "val", "signaled", "noinline")

    def __init__(self, eng, fn, is_async, inc):
        self.eng = eng
        self.fn = fn
        self.deps = []
        self.is_async = is_async
        self.inc = inc
        self.sem = None
        self.val = 0
        self.signaled = is_async
        self.noinline = False


class Sched:
    def __init__(self, nc):
        self.nc = nc
        self.ops = {e: [] for e in ENGS}
        self.last_w = {}
        self.readers = {}
        self.akeys = {}

    def alias(self, new_keys, old_keys):
        olds = []
        seen = set()
        for k in old_keys:
            w = self.last_w.get(k)
            for o in ([w] if w is not None else []) + list(self.readers.get(k, ())):
                if id(o) not in seen:
                    seen.add(id(o))
                    olds.append(o)
        for k in new_keys:
            self.last_w[k] = None
            self.readers[k] = list(olds)

    def op(self, eng, fn, reads=(), writes=(), akey=None, inc=16, chain=False, noinline=False):
        o = Op(eng, fn, akey is not None, inc)
        o.noinline = noinline
        sem = ("a", akey) if akey is not None else None
        deps = []
        for k in reads:
            w = self.last_w.get(k)
            if w is not None:
                deps.append(w)
        for k in writes:
            w = self.last_w.get(k)
            if w is not None and not (chain and w.sem == sem):
                deps.append(w)
            deps.extend(self.readers.get(k, ()))
        seen = set()
        for d in deps:
            if id(d) in seen:
                continue
            seen.add(id(d))
            if (not d.is_async) and d.eng == eng and eng == "pe":
                continue
            o.deps.append(d)
            d.signaled = True
        if akey is not None:
            o.sem = sem
            self.akeys[akey] = self.akeys.get(akey, 0) + inc
            o.val = self.akeys[akey]
        for k in writes:
            self.last_w[k] = o
            self.readers[k] = []
        for k in reads:
            if k not in writes:
                lst = self.readers.setdefault(k, [])
                if not o.is_async:
                    lst[:] = [r for r in lst if r.is_async or r.eng != eng]
                lst.append(o)
        self.ops[eng].append(o)
        return o

    def emit(self):
        nc = self.nc
        for e in ENGS:
            c = 0
            for o in self.ops[e]:
                if o.is_async:
                    continue
                if o.signaled:
                    c += 1
                    o.sem = ("e", e)
                    o.val = c
        names = [("e", e) for e in ENGS] + [("a", k) for k in self.akeys]
        sems = {}
        for i, sn in enumerate(names):
            sems[sn] = nc.alloc_semaphore(name="s%d" % i)
        self.n_sems = len(names)
        final = {("a", k): v for k, v in self.akeys.items()}
        ops = self.ops

        def run_stream(e, engine):
            waited = {}
            for o in ops[e]:
                need = {}
                for d in o.deps:
                    if d.val > waited.get(d.sem, 0) and d.val > need.get(d.sem, 0):
                        need[d.sem] = d.val
                items = list(need.items())
                inl = None
                if items and e in ("act", "dve", "pool") and not o.is_async and not o.noinline:
                    inl = items.pop()
                for sm, v in items:
                    engine.wait_ge(sems[sm], v)
                    waited[sm] = v
                ins = o.fn(engine)
                if inl is not None:
                    ins.wait_op(sems[inl[0]], inl[1], "sem-ge")
                    waited[inl[0]] = inl[1]
                if o.is_async:
                    ins.then_inc(sems[o.sem], o.inc)
                elif o.signaled:
                    ins.then_inc(sems[o.sem], 1)
            if e == "sp":
                for k, v in final.items():
                    if waited.get(k, 0) < v:
                        engine.wait_ge(sems[k], v)

        with nc.Block() as block:
            @block.tensor
            def _(eng):
                run_stream("pe", eng)

            @block.scalar
            def _(eng):
                run_stream("act", eng)

            @block.vector
            def _(eng):
                run_stream("dve", eng)

            @block.gpsimd
            def _(eng):
                run_stream("pool", eng)

            @block.sync
            def _(eng):
                run_stream("sp", eng)


def build(stop_after=3, dbg=False, skip12=False, p3_limit=99):
    nc = bass.Bass("TRN2", target_bir_lowering=False)
    S = Sched(nc)

    def din(name, shape, dt=F32):
        return nc.dram_tensor(name, list(shape), dt, kind="ExternalInput").ap()

    xb = din("xb", [SEQ, D])
    xs = din("xs", [OWN, D])
    wqkv = din("wqkv", [D, 768])
    wgate = din("wgate", [D, 2 * D])
    wsb = din("wsb", [512, D])
    wdf = din("wdf", [512, D])
    wo = din("wo", [D, D])
    wup = din("wup", [D, DFF])
    wdn = din("wdn", [DFF, D])
    gvec = din("gvec", [4, D])
    lamv = din("lamv", [1, 256])
    subln = din("subln", [128, 1])
    btd = din("bt", [128, 5 * 512])
    c15d = din("c15", [128, 1])
    ohd = din("onehot", [128, 4])
    out = nc.dram_tensor("out", [OWN, D], F32, kind="ExternalOutput").ap()
    rs_in_l = [nc.dram_tensor("rs_in0", [4 * 1024, 1536], BF16), nc.dram_tensor("rs_in1", [4 * 1024, 512], BF16)]
    rs_out_l = [nc.dram_tensor("rs_out0", [1024, 1536], BF16), nc.dram_tensor("rs_out1", [1024, 512], BF16)]
    GROUPS = [[qt for qt in range(NQT) if qt % 4 != 3], [qt for qt in range(NQT) if qt % 4 == 3]]
    cur = {"gi": 0}
    if dbg:
        d_qk = nc.dram_tensor("d_qk", [4, 128, SEQ], BF16, kind="ExternalOutput").ap()
        d_v = nc.dram_tensor("d_v", [2, 128, NKB * 128], BF16, kind="ExternalOutput").ap()
        d_rs = nc.dram_tensor("d_rs", [1024, OWN], BF16, kind="ExternalOutput").ap()

    arena = nc.alloc_sbuf_tensor("arena", [128, 212832], mybir.dt.uint8)
    ABASE = 16512
    assert nc.sbuf_base >= ABASE + 212832 - 64, (nc.sbuf_base,)

    def sb(name, shape, dt, off):
        return nc.alloc_sbuf_tensor_at(name, list(shape), dt, offset=ABASE + off)

    ps = nc.alloc_psum_tensor("ps", [128, 8 * 512], F32)
    psb16 = ps.bitcast(BF16)

    def bank(i, lo=0, hi=512, p0=0, p1=128):
        return ps[p0:p1, i * 512 + lo:i * 512 + hi]

    def pk(i):
        return ("pb", i)

    R0, C0, T0 = 0, 98304, 100352
    QT_sb = sb("QT_sb", [128, SEQ], BF16, R0 + 0)
    KT_sb = sb("KT_sb", [128, SEQ], BF16, R0 + 16384)
    QT_d = sb("QT_d", [128, SEQ], BF16, R0 + 32768)
    QT_sbB = sb("QT_sbB", [128, SEQ], BF16, T0 + 77824)
    QT_dB = sb("QT_dB", [128, SEQ], BF16, T0 + 94208)
    QTs = [QT_sb, QT_sbB]
    QTd = [QT_d, QT_dB]
    KT_d = sb("KT_d", [128, SEQ], BF16, R0 + 49152)
    V_sb = sb("V_sb", [128, NKB, 128], BF16, R0 + 65536)
    V_d = sb("V_d", [128, NKB, 128], BF16, R0 + 81920)
    ident = sb("ident", [128, 128], BF16, C0 + 0)
    uneg = sb("uneg", [128, 128], BF16, C0 + 256)
    ones = sb("ones", [128, 128], BF16, C0 + 512)
    onesm = sb("onesm", [128, 128], BF16, C0 + 768)
    mtri = sb("mtri", [128, 128], BF16, C0 + 1024)
    onehot = sb("onehot_sb", [128, 4], F32, C0 + 1280)
    c15 = sb("c15_sb", [128, 1], F32, C0 + 1312)
    sublnS = sb("subln_sb", [128, 1], F32, C0 + 1344)
    lamS = sb("lam_sb", [128, 8], F32, C0 + 1376)
    stat = sb("stat", [128, 32], F32, C0 + 1408)
    cf32 = sb("cf32", [128, 128], F32, C0 + 1536)

    def acts(fn, reads, writes):
        return S.op("act", fn, reads, writes)

    def build_const(dst, val, sel):
        key = "cf32"
        S.op("pool", lambda e: e.memset(cf32[:, :], val), writes=[key])
        if sel is not None:
            S.op("pool", lambda e: e.affine_select(out=cf32[:, :], in_=cf32[:, :], pattern=[[-1, 128]],
                                                   compare_op=sel, fill=0.0, base=0, channel_multiplier=1),
                 reads=[key], writes=[key])
        S.op("dve", lambda e: e.tensor_copy(out=dst[:, :], in_=cf32[:, :]), reads=[key], writes=[dst.name])

    QZ = ["qz0", "qz1", "qz2", "qz3"]
    build_const(ident, 1.0, ALU.is_equal)
    build_const(uneg, -1.0, ALU.is_ge)
    build_const(mtri, NEG, ALU.is_ge)
    build_const(onesm, 1.0 / 128.0, None)
    build_const(ones, 1.0, None)

    xsl = [sb("xsl%d" % i, [128, D], F32, T0 + 4096 * i) for i in range(4)]
    junk = sb("junk", [128, D], BF16, T0 + 16384)
    xn = [sb("xn%d" % i, [128, D], BF16, T0 + 18432 + 2048 * i) for i in range(2)]
    hT = [sb("hT%d" % i, [128, 8, 512], BF16, T0 + 22528 + 8192 * i) for i in range(2)]
    wq = sb("wq", [128, 8, 768], BF16, T0 + 38912)
    gbc0 = sb("gbc0", [128, D], F32, T0 + 51200)
    lamB = sb("lamB", [128, 256], F32, T0 + 55296)
    lamP = sb("lamP", [128, 128], F32, T0 + 56320)
    P1_KEYS = ["xsl%d" % i for i in range(4)] + ["junk", "xn0", "xn1", "hT0", "hT1", "wq", "gbc0", "lamB", "lamP"] + ["xnq%d" % i for i in range(8)]

    cst = "consts"
    S.op("sp", lambda e: e.dma_start(out=onehot[:, :], in_=ohd[:, :]), writes=[cst], akey=cst, chain=True)
    S.op("sp", lambda e: e.dma_start(out=c15[:, :], in_=c15d[:, :]), writes=[cst], akey=cst, chain=True)
    S.op("sp", lambda e: e.dma_start(out=sublnS[:, :], in_=subln[:, :]), writes=[cst], akey=cst, chain=True)
    S.op("sp", lambda e: e.dma_start(out=gbc0[:, :], in_=gvec[0:1, :].partition_broadcast(128)), writes=[cst], akey=cst, chain=True)
    S.op("sp", lambda e: e.dma_start(out=lamB[:, :], in_=lamv[0:1, :].partition_broadcast(128)), writes=[cst], akey=cst, chain=True)
    S.op("pool", lambda e: e.dma_start(out=wq[:, :, :], in_=wqkv.rearrange("(kc p) n -> p kc n", p=128)),
         writes=["wq"], akey="wq")
    S.op("pool", lambda e: e.memset(QT_sb[64:128, :], 0.0), writes=["qz0"])
    S.op("pool", lambda e: e.memset(QT_sbB[0:64, :], 0.0), writes=["qz1"])
    S.op("pool", lambda e: e.memset(QT_d[64:128, :], 0.0), writes=["qz2"])
    S.op("pool", lambda e: e.memset(QT_dB[0:64, :], 0.0), writes=["qz3"])

    S.op("dve", lambda e: e.tensor_tensor(out=lamP[:, 0:64], in0=lamB[:, 0:64], in1=lamB[:, 64:128], op=ALU.mult), reads=[cst], writes=["lamP"])
    S.op("dve", lambda e: e.tensor_tensor(out=lamP[:, 64:128], in0=lamB[:, 128:192], in1=lamB[:, 192:256], op=ALU.mult), reads=[cst, "lamP"], writes=["lamP"])
    S.op("dve", lambda e: e.reduce_sum(out=lamS[:, 0:1], in_=lamP[:, 0:64], axis=AX.X), reads=["lamP"], writes=["lamS"])
    S.op("dve", lambda e: e.reduce_sum(out=lamS[:, 1:2], in_=lamP[:, 64:128], axis=AX.X), reads=["lamP", "lamS"], writes=["lamS"])
    S.op("act", lambda e: e.activation(out=lamS[:, 2:4], in_=lamS[:, 0:2], func=AF.Exp), reads=["lamS"], writes=["lamS"])
    S.op("dve", lambda e: e.tensor_tensor(out=lamS[:, 4:5], in0=lamS[:, 3:4], in1=lamS[:, 2:3], op=ALU.subtract), reads=["lamS"], writes=["lamS"])
    S.op("dve", lambda e: e.tensor_scalar(out=lamS[:, 5:6], in0=lamS[:, 4:5], scalar1=-LAM_INIT, scalar2=None, op0=ALU.add), reads=["lamS"], writes=["lamS"])
    S.op("dve", lambda e: e.tensor_scalar(out=sublnS[:, :], in0=sublnS[:, :], scalar1=1.0 - LAM_INIT, scalar2=None, op0=ALU.mult), reads=[cst], writes=["sublnS"])
    neglam = lamS[:, 5:6]

    cnt = {"stat": 0, "xn": 0}

    def norm_transpose(x_ap, xkey, gbc, gkeys, hT_t, hkey, s, tr_banks, nb=None):
        if nb is None:
            nb = (junk, "junk", xn, ["xn0", "xn1"])
        junk_t, junk_k, xn_l, xn_k = nb
        c = (cnt["stat"] % 8) * 4
        cnt["stat"] += 1
        skey = ("stat", c)
        S.op("act", lambda e: e.activation(out=junk_t[:, :], in_=x_ap, func=AF.Square, accum_out=stat[:, c:c + 1]),
             reads=[xkey], writes=[junk_k, skey], noinline=True)
        S.op("act", lambda e: e.activation(out=stat[:, c + 1:c + 2], in_=stat[:, c:c + 1], func=AF.Ln, scale=1.0 / D, bias=EPS),
             reads=[skey], writes=[skey])
        S.op("act", lambda e: e.activation(out=stat[:, c + 2:c + 3], in_=stat[:, c + 1:c + 2], func=AF.Exp, scale=-0.5),
             reads=[skey], writes=[skey])
        xi = cnt["xn"] % 2
        cnt["xn"] += 1
        xt = xn_l[xi]
        S.op("dve", lambda e: e.scalar_tensor_tensor(out=xt[:, :], in0=x_ap, scalar=stat[:, c + 2:c + 3], in1=gbc[:, :],
                                                     op0=ALU.mult, op1=ALU.mult),
             reads=[xkey, skey] + gkeys, writes=[xn_k[xi]])
        for kc in range(8):
            bi = tr_banks[kc // 2]
            o_ap = psb16[:, bi * 1024 + (kc % 2) * 512 + s * 128: bi * 1024 + (kc % 2) * 512 + (s + 1) * 128]
            S.op("pe", lambda e, o_ap=o_ap, kc=kc: e.transpose(o_ap, xt[:, kc * 128:(kc + 1) * 128], ident[:, :]),
                 reads=[xn_k[xi], "ident"], writes=[pk(bi)])

    def norm_only(x_ap, xkey, gbc, gkeys, xt, xk, junk_t, junk_k):
        c = (cnt["stat"] % 8) * 4
        cnt["stat"] += 1
        skey = ("stat", c)
        S.op("act", lambda e: e.activation(out=junk_t[:, :], in_=x_ap, func=AF.Square, accum_out=stat[:, c:c + 1]),
             reads=[xkey], writes=[junk_k, skey], noinline=True)
        S.op("act", lambda e: e.activation(out=stat[:, c + 1:c + 2], in_=stat[:, c:c + 1], func=AF.Ln, scale=1.0 / D, bias=EPS),
             reads=[skey], writes=[skey])
        S.op("act", lambda e: e.activation(out=stat[:, c + 2:c + 3], in_=stat[:, c + 1:c + 2], func=AF.Exp, scale=-0.5),
             reads=[skey], writes=[skey])
        S.op("dve", lambda e: e.scalar_tensor_tensor(out=xt[:, :], in0=x_ap, scalar=stat[:, c + 2:c + 3], in1=gbc[:, :],
                                                     op0=ALU.mult, op1=ALU.mult),
             reads=[xkey, skey] + gkeys, writes=[xk])

    def tr_only(xt, xk, s, tr_banks):
        for kc in range(8):
            bi = tr_banks[kc // 2]
            o_ap = psb16[:, bi * 1024 + (kc % 2) * 512 + s * 128: bi * 1024 + (kc % 2) * 512 + (s + 1) * 128]
            S.op("pe", lambda e, o_ap=o_ap, kc=kc: e.transpose(o_ap, xt[:, kc * 128:(kc + 1) * 128], ident[:, :]),
                 reads=[xk, "ident"], writes=[pk(bi)])

    def evac_transposes(hT_t, hkey, tr_banks):
        for i, bi in enumerate(tr_banks):
            src = psb16[:, bi * 1024:(bi + 1) * 1024]
            dst = hT_t[:, 2 * i:2 * i + 2, :]
            if i % 2 == 0:
                S.op("act", lambda e, src=src, dst=dst: e.copy(out=dst, in_=src.rearrange("p (a n) -> p a n", a=2)),
                     reads=[pk(bi)], writes=[hkey])
            else:
                S.op("dve", lambda e, src=src, dst=dst: e.tensor_copy(out=dst, in_=src.rearrange("p (a n) -> p a n", a=2)),
                     reads=[pk(bi)], writes=[hkey])

    xcnt = [0]

    def load_x(src_ap):
        i = xcnt[0] % 4
        xcnt[0] += 1
        S.op("sp", lambda e: e.dma_start(out=xsl[i][:, :], in_=src_ap), writes=["xsl%d" % i], akey="xsl%d" % i)
        return i

    xnq = [sb("xnq%d" % i, [128, D], BF16, T0 + 57344 + 2048 * i) for i in range(8)]

    def p1_norm(t, subs=(0, 1, 2, 3)):
        for s in subs:
            i = load_x(xb[(t * 4 + s) * 128:(t * 4 + s + 1) * 128, :])
            q = (t % 2) * 4 + s
            norm_only(xsl[i][:, :], "xsl%d" % i, gbc0, [cst], xnq[q], "xnq%d" % q, junk, "junk")

    def p1_front(t):
        slot = t % 2
        for s in range(4):
            q = (t % 2) * 4 + s
            tr_only(xnq[q], "xnq%d" % q, s, [0, 1, 2, 3])
        evac_transposes(hT[slot], "hT%d" % slot, [0, 1, 2, 3])

    def p1_back(t, between=None):
        slot = t % 2
        hk = "hT%d" % slot
        dsts = [(QTs, "QT_sb", 0.125), (KT_sb, "KT_sb", None), (QTd, "QT_d", 0.125), (KT_d, "KT_d", None)]
        for cg in range(4):
            if between is not None:
                between(cg)
            bi = 4 + cg % 2
            for kc in range(8):
                S.op("pe", lambda e, cg=cg, kc=kc, bi=bi: e.matmul(bank(bi), lhsT=wq[:, kc, cg * 128:(cg + 1) * 128], rhs=hT[slot][:, kc, :],
                                                              start=(kc == 0), stop=(kc == 7)),
                     reads=["wq", hk], writes=[pk(bi)])
            dst, dk, sc = dsts[cg]
            if sc is not None:
                for hh in range(2):
                    d_ap = dst[hh][64 * hh:64 * hh + 64, t * 512:(t + 1) * 512]
                    S.op("act", lambda e, d_ap=d_ap, bi=bi, sc=sc, hh=hh: e.activation(out=d_ap, in_=bank(bi, 0, 512, 64 * hh, 64 * hh + 64), func=AF.Copy, scale=sc),
                         reads=[pk(bi)] + QZ, writes=[(dk, t)])
            else:
                d_ap = dst[:, t * 512:(t + 1) * 512]
                S.op("dve", lambda e, d_ap=d_ap, bi=bi: e.tensor_copy(out=d_ap, in_=bank(bi)),
                     reads=[pk(bi)], writes=[(dk, t)])
        for s in range(4):
            bi = 6 + s % 2
            for kc in range(8):
                S.op("pe", lambda e, s=s, kc=kc, bi=bi: e.matmul(bank(bi, 0, 256), lhsT=hT[slot][:, kc, s * 128:(s + 1) * 128], rhs=wq[:, kc, 512:768],
                                                            start=(kc == 0), stop=(kc == 7)),
                     reads=["wq", hk], writes=[pk(bi)])
            kb = t * 4 + s
            if s % 2 == 0:
                S.op("dve", lambda e, kb=kb, bi=bi: e.tensor_copy(out=V_sb[:, kb, :], in_=bank(bi, 0, 128)), reads=[pk(bi)], writes=[("V_sb", kb)])
                S.op("dve", lambda e, kb=kb, bi=bi: e.tensor_copy(out=V_d[:, kb, :], in_=bank(bi, 128, 256)), reads=[pk(bi)], writes=[("V_d", kb)])
            else:
                S.op("act", lambda e, kb=kb, bi=bi: e.copy(out=V_sb[:, kb, :], in_=bank(bi, 0, 128)), reads=[pk(bi)], writes=[("V_sb", kb)])
                S.op("act", lambda e, kb=kb, bi=bi: e.copy(out=V_d[:, kb, :], in_=bank(bi, 128, 256)), reads=[pk(bi)], writes=[("V_d", kb)])

    for t in range(NQT + 1):
        if skip12:
            break
        if t == 0:
            p1_norm(0)
        if t < NQT:
            p1_front(t)
        nxt = (lambda cg, t=t: p1_norm(t + 1, (cg,))) if t + 1 < NQT else None
        if t >= 1:
            p1_back(t - 1, nxt)
        elif nxt is not None:
            for cg in range(4):
                nxt(cg)

    R_ATT = [(n, t) for n in ("QT_sb", "KT_sb", "QT_d", "KT_d") for t in range(NQT)] + \
            [(n, kb) for n in ("V_sb", "V_d") for kb in range(NKB)]

    if dbg:
        for i, (tn, nm) in enumerate(((QT_sb, "QT_sb"), (KT_sb, "KT_sb"), (QT_d, "QT_d"), (KT_d, "KT_d"))):
            S.op("sp", lambda e, i=i, tn=tn: e.dma_start(out=d_qk[i], in_=tn[:, :]), reads=[(nm, t) for t in range(NQT)], writes=[("dbg", i)], akey="dbg%d" % i)
        S.op("sp", lambda e: e.dma_start(out=d_v[0], in_=V_sb[:, :, :].rearrange("p a b -> p (a b)")), reads=[("V_sb", k) for k in range(NKB)], writes=[("dbg", 4)], akey="dbg4")
        S.op("sp", lambda e: e.dma_start(out=d_v[1], in_=V_d[:, :, :].rearrange("p a b -> p (a b)")), reads=[("V_d", k) for k in range(NKB)], writes=[("dbg", 5)], akey="dbg5")

    if stop_after >= 2 and not skip12:
        eB = [sb("eB%d" % i, [128, 512], F32, T0 + 2048 * i) for i in range(2)]
        LpB = [sb("LpB%d" % i, [128, 512], BF16, T0 + 4096 + 1024 * i) for i in range(2)]
        argB = [sb("argB%d" % i, [128, 512], F32, T0 + 6144 + 2048 * i) for i in range(2)]
        wB = [sb("wB%d" % i, [128, 512], BF16, T0 + 10240 + 1024 * i) for i in range(2)]
        carry = sb("carry", [128, 512], F32, T0 + 12288)
        stg = [[sb("stg%d_%d" % (s_, i), [128, 512], BF16, T0 + 14336 + 4096 * s_ + 1024 * i) for i in range(4)] for s_ in range(2)]
        aB = [[sb("aB%d_%d" % (s_, m), [128, 512], F32, T0 + 22528 + 4096 * s_ + 2048 * m) for m in range(2)] for s_ in range(2)]
        dwB = [[sb("dwB%d_%d" % (s_, m), [128, 512], BF16, T0 + 30720 + 2048 * s_ + 1024 * m) for m in range(2)] for s_ in range(2)]
        BT = sb("BT", [128, 5 * 512], F32, T0 + 34816)
        pvs = [sb("pvs%d" % m, [128, 512], F32, T0 + 45056 + 2048 * m) for m in range(2)]
        lsS = [sb("lsS%d" % m, [128, 512], F32, T0 + 49152 + 2048 * m) for m in range(2)]
        rB = [sb("rB%d" % m, [128, 512], F32, T0 + 53248 + 2048 * m) for m in range(2)]
        t1B = sb("t1B", [128, 512], F32, T0 + 57344)
        yvB = sb("yvB", [128, 512], F32, T0 + 59392)
        ysqB = sb("ysqB", [128, 512], BF16, T0 + 61440)
        lnvB = sb("lnvB", [128, 512], F32, T0 + 62464)
        rstB = sb("rstB", [128, 512], F32, T0 + 64512)
        ynB = sb("ynB", [128, 512], F32, T0 + 66560)
        dstg = [[sb("dstg%d_%d" % (s_, i), [128, 512], BF16, T0 + 68608 + 4096 * s_ + 1024 * i) for i in range(4)] for s_ in range(2)]
        P2_KEYS = (["eB0", "eB1", "LpB0", "LpB1", "argB0", "argB1", "wB0", "wB1", "carry", "BT",
                    "pvs0", "pvs1", "lsS0", "lsS1", "rB0", "rB1", "t1B", "yvB", "ysqB", "lnvB", "rstB", "ynB"]
                   + ["stg%d_%d" % (s_, i) for s_ in range(2) for i in range(4)]
                   + ["dstg%d_%d" % (s_, i) for s_ in range(2) for i in range(4)]
                   + ["aB%d_%d" % (s_, m) for s_ in range(2) for m in range(2)]
                   + ["dwB%d_%d" % (s_, m) for s_ in range(2) for m in range(2)])
        P2_KEYS = P2_KEYS + ["aP0", "aP1", "dwP0", "dwP1", ("wacc", 0), ("wacc", 1), "eP0", "eP1", "LpP0", "LpP1", "wP0", "wP1", "carryP", "sbev"] + ["stgP%d" % i for i in range(4)]
        S.alias(P2_KEYS, P1_KEYS)
        S.op("sp", lambda e: e.dma_start(out=BT[:, :], in_=btd[:, :]), writes=["BT"], akey="bt")

        rs_keys = []

        tiles = []

        eP = [sb("eP%d" % i, [128, 2, 512], F32, T0 + 4096 * i) for i in range(2)]
        LpP = [sb("LpP%d" % i, [128, 2, 512], BF16, T0 + 8192 + 2048 * i) for i in range(2)]
        wP = [sb("wP%d" % i, [128, 2, 512], BF16, T0 + 12288 + 2048 * i) for i in range(2)]
        stgP = [sb("stgP%d" % i, [128, 512], BF16, T0 + 16384 + 1024 * i) for i in range(4)]
        carryP = sb("carryP", [128, 2, 512], F32, T0 + 53248)
        sbev = sb("sbev", [128, 512], F32, T0 + 20480)

        def psP(b0, cs):
            return ps[:, b0 * 512:(b0 + 2) * 512].rearrange("p (m n) -> p m n", m=2)[:, :, cs:512]

        def sb_z(i):
            T = tiles[i]
            s_ = i % 2
            cs = T["cs"]
            q0 = T["qt"] * 512
            k0 = T["kb"] * 128
            diag = T["j"] >= 0
            for h in range(2):
                S.op("pe", lambda e, h=h: e.matmul(bank(h, cs, 512), lhsT=KT_sb[:, k0:k0 + 128], rhs=QTs[h][:, q0 + cs:q0 + 512],
                                                   start=True, stop=not diag),
                     reads=[("KT_sb", T["kb"] // 4), ("QT_sb", T["qt"])], writes=[pk(h)])
                if diag:
                    S.op("pe", lambda e, h=h: e.matmul(bank(h, cs, cs + 128), lhsT=ident[:, :], rhs=mtri[:, :], start=False, stop=True),
                         reads=["ident", "mtri"], writes=[pk(h)])
            S.op("act", lambda e: e.activation(out=eP[s_][:, :, cs:512], in_=psP(0, cs), func=AF.Exp),
                 reads=[pk(0), pk(1)], writes=["eP%d" % s_])
            S.op("act", lambda e: e.activation(out=LpP[s_][:, :, cs:512], in_=eP[s_][:, :, cs:512], func=AF.Ln, bias=1.0, scale=1.0),
                 reads=["eP%d" % s_], writes=["LpP%d" % s_])

        def sb_mid(i):
            T = tiles[i]
            s_ = i % 2
            cs = T["cs"]
            q0 = T["qt"] * 512
            k0 = T["kb"] * 128
            diag = T["j"] >= 0
            for h in range(2):
                bB = 2 + h
                S.op("pe", lambda e, h=h, bB=bB: e.matmul(bank(bB, cs, 512), lhsT=uneg[:, :], rhs=LpP[s_][:, h, cs:512], start=True, stop=False),
                     reads=["uneg", "LpP%d" % s_], writes=[pk(bB)])
                S.op("pe", lambda e, h=h, bB=bB: e.matmul(bank(bB, cs, 512), lhsT=KT_sb[:, k0:k0 + 128], rhs=QTs[h][:, q0 + cs:q0 + 512],
                                                          start=False, stop=not diag),
                     reads=[("KT_sb", T["kb"] // 4), ("QT_sb", T["qt"])], writes=[pk(bB)])
                if diag:
                    S.op("pe", lambda e, bB=bB: e.matmul(bank(bB, cs, cs + 128), lhsT=ident[:, :], rhs=mtri[:, :], start=False, stop=True),
                         reads=["ident", "mtri"], writes=[pk(bB)])
            if not T["last"]:
                for h in range(2):
                    bC = 4 + h
                    S.op("pe", lambda e, h=h, bC=bC: e.matmul(bank(bC, cs, 512), lhsT=ones[:, :], rhs=LpP[s_][:, h, cs:512], start=True, stop=True),
                         reads=["ones", "LpP%d" % s_], writes=[pk(bC)])
            if T["first"]:
                S.op("dve", lambda e: e.memset(carryP[:, :, :], 0.0), writes=["carryP"])
            S.op("dve", lambda e: e.tensor_tensor(out=eP[s_][:, :, cs:512], in0=psP(2, cs), in1=carryP[:, :, cs:512], op=ALU.subtract),
                 reads=[pk(2), pk(3), "carryP"], writes=["eP%d" % s_])
            if not T["last"]:
                S.op("dve", lambda e: e.tensor_tensor(out=carryP[:, :, cs:512], in0=psP(4, cs), in1=carryP[:, :, cs:512], op=ALU.add),
                     reads=[pk(4), pk(5), "carryP"], writes=["carryP"])
            S.op("act", lambda e: e.activation(out=wP[s_][:, :, cs:512], in_=eP[s_][:, :, cs:512], func=AF.Exp),
                 reads=["eP%d" % s_], writes=["wP%d" % s_])

        def sb_pv(i):
            T = tiles[i]
            s_ = i % 2
            cs = T["cs"]
            kb = T["kb"]
            qt = T["qt"]
            for h in range(2):
                S.op("pe", lambda e, h=h: e.matmul(bank(6 + h, cs, 512), lhsT=V_sb[:, kb, :], rhs=wP[s_][:, h, cs:512],
                                                   start=T["first"], stop=T["last"], skip_group_check=True),
                     reads=[("V_sb", kb), "wP%d" % s_], writes=[pk(6 + h)])
            if T["last"]:
                for h in range(2):
                    S.op("dve", lambda e, h=h: e.tensor_copy(out=sbev[64 * h:64 * h + 64, :], in_=bank(6 + h, 0, 512, 64 * h, 64 * h + 64)),
                         reads=[pk(6 + h)] + (["sbev"] if h == 1 else []), writes=["sbev"])
                for i4 in range(4):
                    st = stgP[i4]
                    skey = "stgP%d" % i4
                    S.op("pool", lambda e, st=st, i4=i4: e.tensor_scalar(out=st[:, :], in0=sbev[:, :], scalar1=onehot[:, i4:i4 + 1], scalar2=None, op0=ALU.mult),
                         reads=["sbev", cst], writes=[skey])
                    r0 = (qt // 4) * 1024 + 128 * i4
                    c0 = (qt % 4) * 512 if cur["gi"] == 0 else 0
                    rin = rs_in_l[cur["gi"]]
                    rk = ("rs_in", 0, qt, i4)
                    rs_keys.append(rk)
                    S.op("sp", lambda e, st=st, r0=r0, c0=c0, rin=rin: e.dma_start(out=rin[r0:r0 + 128, c0:c0 + 512], in_=st[:, :]),
                         reads=[skey], writes=[rk], akey=skey)

        dt_tiles = []
        dcnt = [0]

        waccT = sb("waccT", [128, 2, 512], F32, T0 + 53248)
        aP = [sb("aP%d" % s_, [128, 2, 512], F32, T0 + 22528 + 4096 * s_) for s_ in range(2)]
        dwP = [sb("dwP%d" % s_, [128, 2, 512], BF16, T0 + 30720 + 2048 * s_) for s_ in range(2)]

        def pair_ps(s_, cs):
            return ps[:, (2 * s_) * 512:(2 * s_ + 2) * 512].rearrange("p (m n) -> p m n", m=2)[:, :, cs:512]

        def df_z(i):
            T = dt_tiles[i]
            s_ = i % 2
            cs = T["cs"]
            q0 = T["qt"] * 512
            k0 = T["kb"] * 128
            wk = "dwP%d" % s_
            for m in range(2):
                bz = 2 * s_ + m
                S.op("pe", lambda e, m=m, bz=bz: e.matmul(bank(bz, cs, 512), lhsT=KT_d[:, k0:k0 + 128],
                                                          rhs=QTd[m][:, q0 + cs:q0 + 512], start=True, stop=True),
                     reads=[("KT_d", T["kb"] // 4), ("QT_d", T["qt"])], writes=[pk(bz)])
            if T["near"]:
                jj = T["j"] + 1
                ak = "aP%d" % s_
                for m in range(2):
                    bz = 2 * s_ + m
                    S.op("dve", lambda e, m=m, bz=bz, jj=jj: e.tensor_tensor(out=aP[s_][:, m, cs:512], in0=bank(bz, cs, 512),
                                                                         in1=BT[:, jj * 512 + cs:(jj + 1) * 512], op=ALU.add),
                         reads=[pk(bz), "BT"] + ([ak] if m == 1 else []), writes=[ak])
                S.op("act", lambda e: e.activation(out=dwP[s_][:, :, cs:512], in_=aP[s_][:, :, cs:512], func=AF.Exp),
                     reads=[ak], writes=[wk])
            else:
                S.op("act", lambda e: e.activation(out=dwP[s_][:, :, :], in_=pair_ps(s_, 0), func=AF.Exp, bias=c15[:, 0:1], scale=1.0),
                     reads=[pk(2 * s_), pk(2 * s_ + 1), cst], writes=[wk])

        dpend = []

        def df_flush(force):
            for ent in list(dpend):
                ent[0] -= 1
                if force or ent[0] <= 0:
                    dpend.remove(ent)
                    ent[1]()

        def df_pv(i):
            T = dt_tiles[i]
            s_ = i % 2
            cs = T["cs"]
            kb = T["kb"]
            qt = T["qt"]
            wk = "dwP%d" % s_
            df_flush(T["last"])
            for m in range(2):
                S.op("pe", lambda e, m=m: e.matmul(bank(4 + m, cs, 512), lhsT=V_d[:, kb, :], rhs=dwP[s_][:, m, cs:512],
                                                   start=T["first"], stop=T["last"], skip_group_check=True),
                     reads=[("V_d", kb), wk], writes=[pk(4 + m)])
                S.op("pe", lambda e, m=m: e.matmul(bank(6 + m, cs, 512), lhsT=ones[:, :], rhs=dwP[s_][:, m, cs:512],
                                                   start=T["first"], stop=T["last"], skip_group_check=True),
                     reads=["ones", wk], writes=[pk(6 + m)])
            if T["last"]:
                S.op("dve", lambda e: e.tensor_copy(out=pvs[0][:, :], in_=bank(4)), reads=[pk(4)], writes=["pvs0"])
                S.op("dve", lambda e: e.tensor_copy(out=pvs[1][:, :], in_=bank(5)), reads=[pk(5)], writes=["pvs1"])
                S.op("act", lambda e: e.copy(out=lsS[0][:, :], in_=bank(6)), reads=[pk(6)], writes=["lsS0"])
                S.op("act", lambda e: e.copy(out=lsS[1][:, :], in_=bank(7)), reads=[pk(7)], writes=["lsS1"])
                for m in range(2):
                    S.op("dve", lambda e, m=m: e.reciprocal(out=lsS[m][:, :], in_=lsS[m][:, :]), reads=["lsS%d" % m], writes=["lsS%d" % m])
                S.op("dve", lambda e: e.tensor_tensor(out=t1B[:, :], in0=pvs[0][:, :], in1=lsS[0][:, :], op=ALU.mult), reads=["pvs0", "lsS0"], writes=["t1B"])
                S.op("dve", lambda e: e.tensor_tensor(out=pvs[1][:, :], in0=pvs[1][:, :], in1=lsS[1][:, :], op=ALU.mult), reads=["pvs1", "lsS1"], writes=["pvs1"])
                S.op("dve", lambda e: e.scalar_tensor_tensor(out=yvB[:, :], in0=pvs[1][:, :], scalar=neglam, in1=t1B[:, :], op0=ALU.mult, op1=ALU.add),
                     reads=["pvs1", "t1B", "lamS"], writes=["yvB"])
                S.op("pool", lambda e: e.tensor_tensor(out=ysqB[:, :], in0=yvB[:, :], in1=yvB[:, :], op=ALU.mult), reads=["yvB"], writes=["ysqB"])
                gi_now = cur["gi"]
                dpend.append([7, lambda qt=qt, gi_now=gi_now: df_tail(qt, gi_now)])

        def df_tail(qt, gi_now):
                S.op("pe", lambda e: e.matmul(bank(0), lhsT=onesm[:, :], rhs=ysqB[:, :], start=True, stop=True), reads=["onesm", "ysqB"], writes=[pk(0)])
                S.op("act", lambda e: e.activation(out=lnvB[:, :], in_=bank(0), func=AF.Ln, bias=EPS, scale=1.0), reads=[pk(0)], writes=["lnvB"])
                S.op("act", lambda e: e.activation(out=rstB[:, :], in_=lnvB[:, :], func=AF.Exp, scale=-0.5), reads=["lnvB"], writes=["rstB"])
                S.op("dve", lambda e: e.scalar_tensor_tensor(out=ynB[:, :], in0=yvB[:, :], scalar=sublnS[:, 0:1], in1=rstB[:, :], op0=ALU.mult, op1=ALU.mult),
                     reads=["yvB", "rstB", "sublnS"], writes=["ynB"])
                ss_ = dcnt[0] % 2
                dcnt[0] += 1
                for i4 in range(4):
                    st = dstg[ss_][i4]
                    skey = "dstg%d_%d" % (ss_, i4)
                    S.op("pool", lambda e, st=st, i4=i4: e.tensor_scalar(out=st[:, :], in0=ynB[:, :], scalar1=onehot[:, i4:i4 + 1], scalar2=None, op0=ALU.mult),
                         reads=["ynB", cst], writes=[skey])
                    r0 = (qt // 4) * 1024 + 512 + 128 * i4
                    c0 = (qt % 4) * 512 if gi_now == 0 else 0
                    rin = rs_in_l[gi_now]
                    rk = ("rs_in", 2, qt, i4)
                    rs_keys.append(rk)
                    S.op("sp", lambda e, st=st, r0=r0, c0=c0, rin=rin: e.dma_start(out=rin[r0:r0 + 128, c0:c0 + 512], in_=st[:, :]),
                         reads=[skey], writes=[rk], akey=skey)

        Wg = sb("Wg", [128, 8, 2 * D], BF16, R0 + 0)
        Wo_ = sb("Wo", [128, 8, D], BF16, R0 + 65536)
        for gi, qts in enumerate(GROUPS):
            cur["gi"] = gi
            rs_keys = []
            tiles = []
            for qt in qts:
                for kb in range(4 * qt + 3, -1, -1):
                    j = kb - 4 * qt
                    tiles.append(dict(qt=qt, kb=kb, j=j, cs=max(j, 0) * 128, first=(kb == 4 * qt + 3), last=(kb == 0)))
            N = len(tiles)
            for t in range(N + 2):
                if t < N:
                    sb_z(t)
                if 1 <= t <= N:
                    sb_mid(t - 1)
                if t >= 2:
                    sb_pv(t - 2)
            if gi == len(GROUPS) - 1 and stop_after >= 3:
                S.alias(["Wg"], [(n, t) for n in ("QT_sb", "KT_sb") for t in range(NQT)])
                for kc in range(8):
                    S.op("pool", lambda e, kc=kc: e.dma_start(out=Wg[:, kc, :], in_=wgate[kc * 128:(kc + 1) * 128, :]), writes=["Wg"], akey="Wg", chain=True)
                S.alias(["Wo"], [("V_sb", kb) for kb in range(NKB)])
                for kc in range(8):
                    S.op("pool", lambda e, kc=kc: e.dma_start(out=Wo_[:, kc, :], in_=wo[kc * 128:(kc + 1) * 128, :]), writes=["Wo"], akey="Wo", chain=True)
            dt_tiles = []
            for qt in qts:
                for kb in range(0, 4 * qt + 4):
                    j = kb - 4 * qt
                    dt_tiles.append(dict(qt=qt, kb=kb, j=j, cs=max(j, 0) * 128, first=(kb == 0), last=(kb == 4 * qt + 3), near=(j >= -1)))
            ND = len(dt_tiles)
            for t in range(ND + 1):
                if t < ND:
                    df_z(t)
                if t >= 1:
                    df_pv(t - 1)
            df_flush(True)
            S.op("pool", lambda e, gi=gi: e.collective_compute("ReduceScatter", ALU.add, replica_groups=[[0, 1, 2, 3], [4, 5, 6, 7]],
                                                               ins=[rs_in_l[gi].ap().opt()], outs=[rs_out_l[gi].ap().opt()]),
                 reads=rs_keys, writes=["rs_out%d" % gi], akey="cc%d" % gi, inc=1)
        if dbg:
            S.op("pool", lambda e: e.dma_start(out=d_rs[:, 0:1536], in_=rs_out_l[0][:, :]), reads=["rs_out0"], writes=[("dbg", 6)], akey="dbg6")
            S.op("pool", lambda e: e.dma_start(out=d_rs[:, 1536:2048], in_=rs_out_l[1][:, :]), reads=["rs_out1"], writes=[("dbg", 8)], akey="dbg8")

    if stop_after >= 3:
        Wsb_ = sb("Wsb", [128, 4, D], BF16, R0 + 81920)
        Wdf_ = sb("Wdf", [128, 4, D], BF16, R0 + 90112)
        yT = sb("yT", [128, 8, 512], BF16, R0 + 32768)
        mT = sb("mT", [128, 8, 512], BF16, R0 + 40960)
        gate = [[sb("gate%d_%d" % (p_, i), [128, 512], F32, (R0 + 49152 if p_ == 0 else T0 + 100352) + 2048 * i) for i in range(4)] for p_ in range(2)]
        gbcA0 = sb("gbcA0", [128, D], F32, R0 + 57344)
        gbcA1 = sb("gbcA1", [128, D], F32, R0 + 61440)
        x1all = sb("x1all", [128, 16, D], F32, T0 + 0)
        xsl3 = [sb("xsl3_%d" % i, [128, D], F32, T0 + 65536 + 4096 * i) for i in range(4)]
        junk3 = sb("junk3", [128, D], BF16, T0 + 81920)
        xn3 = [sb("xn3_%d" % i, [128, D], BF16, T0 + 83968 + 2048 * i) for i in range(2)]
        hT3 = sb("hT3", [128, 8, 512], BF16, T0 + 88064)
        tmp3 = sb("tmp3", [128, D], F32, T0 + 96256)
        nb3 = (junk3, "junk3", xn3, ["xn3_0", "xn3_1"])
        SA_R = ["Wg", "Wsb", "Wdf", "Wo", "yT", "mT", "gbcA"] + ["gate0_%d" % i for i in range(4)]
        S.alias([k for k in SA_R if k not in ("Wg", "Wo")], R_ATT)
        if skip12:
            Wg = sb("Wg", [128, 8, 2 * D], BF16, R0 + 0)
            Wo_ = sb("Wo", [128, 8, D], BF16, R0 + 65536)
            for kc in range(8):
                S.op("pool", lambda e, kc=kc: e.dma_start(out=Wg[:, kc, :], in_=wgate[kc * 128:(kc + 1) * 128, :]), writes=["Wg"], akey="Wg", chain=True)
            for kc in range(8):
                S.op("pool", lambda e, kc=kc: e.dma_start(out=Wo_[:, kc, :], in_=wo[kc * 128:(kc + 1) * 128, :]), writes=["Wo"], akey="Wo", chain=True)
        P3_T = ([("x1", i) for i in range(16)] + ["xsl3_%d" % i for i in range(4)] + ["junk3", "xn3_0", "xn3_1", "hT3", "tmp3"]
                + ["gate1_%d" % i for i in range(4)])
        S.alias(P3_T, (P2_KEYS + R_ATT) if not skip12 else [])

        for kc in range(4):
            S.op("pool", lambda e, kc=kc: e.dma_start(out=Wsb_[:, kc, :], in_=wsb[kc * 128:(kc + 1) * 128, :]), writes=["Wsb"], akey="Wsb", chain=True)
        for kc in range(4):
            S.op("pool", lambda e, kc=kc: e.dma_start(out=Wdf_[:, kc, :], in_=wdf[kc * 128:(kc + 1) * 128, :]), writes=["Wdf"], akey="Wdf", chain=True)
        S.op("sp", lambda e: e.dma_start(out=gbcA0[:, :], in_=gvec[0:1, :].partition_broadcast(128)), writes=["gbcA"], akey="gbcA", chain=True)
        S.op("sp", lambda e: e.dma_start(out=gbcA1[:, :], in_=gvec[1:2, :].partition_broadcast(128)), writes=["gbcA"], akey="gbcA", chain=True)

        x3cnt = [0]

        def load_x3(src_ap):
            i = x3cnt[0] % 4
            x3cnt[0] += 1
            S.op("sp", lambda e: e.dma_start(out=xsl3[i][:, :], in_=src_ap), writes=["xsl3_%d" % i], akey="xsl%d" % i)
            return i

        def post_norm_residual(p_, gbc, gkey, res_ap, res_key, dst_ap, dst_key):
            c = (cnt["stat"] % 8) * 4
            cnt["stat"] += 1
            skey = ("stat", c)
            zap = ps[:, 2 * p_ * 512:(2 * p_ + 2) * 512]
            S.op("act", lambda e: e.activation(out=junk3[:, :], in_=zap, func=AF.Square, accum_out=stat[:, c:c + 1]),
                 reads=[pk(2 * p_), pk(2 * p_ + 1)], writes=["junk3", skey], noinline=True)
            S.op("act", lambda e: e.activation(out=stat[:, c + 1:c + 2], in_=stat[:, c:c + 1], func=AF.Ln, scale=1.0 / D, bias=EPS),
                 reads=[skey], writes=[skey])
            S.op("act", lambda e: e.activation(out=stat[:, c + 2:c + 3], in_=stat[:, c + 1:c + 2], func=AF.Exp, scale=-0.5),
                 reads=[skey], writes=[skey])
            S.op("dve", lambda e: e.scalar_tensor_tensor(out=tmp3[:, :], in0=zap, scalar=stat[:, c + 2:c + 3], in1=gbc[:, :], op0=ALU.mult, op1=ALU.mult),
                 reads=[pk(2 * p_), pk(2 * p_ + 1), skey, gkey], writes=["tmp3"])
            S.op("dve", lambda e: e.tensor_tensor(out=dst_ap, in0=tmp3[:, :], in1=res_ap, op=ALU.add),
                 reads=["tmp3", res_key], writes=[dst_key])

        for tt in range(4):
            rsi, rc0 = (0, tt * 512) if tt < 3 else (1, 0)
            S.op("sp", lambda e, rsi=rsi, rc0=rc0: e.dma_start(out=yT[:, :, :], in_=rs_out_l[rsi].ap().rearrange("(kc p) n -> p kc n", p=128)[:, :, rc0:rc0 + 512]),
                 reads=["rs_out%d" % rsi], writes=["yT"], akey="yT")
            xslots = []
            for s in range(4):
                i = load_x3(xs[(tt * 4 + s) * 128:(tt * 4 + s + 1) * 128, :])
                xslots.append(i)
                norm_transpose(xsl3[i][:, :], "xsl3_%d" % i, gbcA0, ["gbcA"], hT3, "hT3", s, [0, 1, 2, 3], nb3)
            evac_transposes(hT3, "hT3", [0, 1, 2, 3])
            if p3_limit <= 1:
                break
            for oc in range(8):
                p_ = oc % 2
                b4 = [4 * p_ + i for i in range(4)]
                gk = ["gate%d_%d" % (p_, i) for i in range(4)]
                gb = gate[p_]
                for kc in range(8):
                    S.op("pe", lambda e, kc=kc, oc=oc, b=b4[2]: e.matmul(bank(b), lhsT=Wg[:, kc, oc * 128:(oc + 1) * 128], rhs=hT3[:, kc, :], start=(kc == 0), stop=(kc == 7)),
                         reads=["Wg", "hT3"], writes=[pk(b4[2])])
                for kc in range(8):
                    S.op("pe", lambda e, kc=kc, oc=oc, b=b4[3]: e.matmul(bank(b), lhsT=Wg[:, kc, D + oc * 128:D + (oc + 1) * 128], rhs=hT3[:, kc, :], start=(kc == 0), stop=(kc == 7)),
                         reads=["Wg", "hT3"], writes=[pk(b4[3])])
                for kc in range(4):
                    S.op("pe", lambda e, kc=kc, oc=oc, b=b4[0]: e.matmul(bank(b), lhsT=Wsb_[:, kc, oc * 128:(oc + 1) * 128], rhs=yT[:, kc, :], start=(kc == 0), stop=(kc == 3)),
                         reads=["Wsb", "yT"], writes=[pk(b4[0])])
                for kc in range(4):
                    S.op("pe", lambda e, kc=kc, oc=oc, b=b4[1]: e.matmul(bank(b), lhsT=Wdf_[:, kc, oc * 128:(oc + 1) * 128], rhs=yT[:, 4 + kc, :], start=(kc == 0), stop=(kc == 3)),
                         reads=["Wdf", "yT"], writes=[pk(b4[1])])
                S.op("act", lambda e, gb=gb, b=b4[2]: e.activation(out=gb[0][:, :], in_=bank(b), func=AF.Sigmoid), reads=[pk(b4[2])], writes=[gk[0]])
                S.op("act", lambda e, gb=gb, b=b4[3]: e.activation(out=gb[1][:, :], in_=bank(b), func=AF.Sigmoid), reads=[pk(b4[3])], writes=[gk[1]])
                S.op("dve", lambda e, gb=gb, b=b4[0]: e.tensor_tensor(out=gb[2][:, :], in0=bank(b), in1=gb[0][:, :], op=ALU.mult), reads=[pk(b4[0]), gk[0]], writes=[gk[2]])
                S.op("dve", lambda e, gb=gb, b=b4[1]: e.tensor_tensor(out=gb[3][:, :], in0=bank(b), in1=gb[1][:, :], op=ALU.mult), reads=[pk(b4[1]), gk[1]], writes=[gk[3]])
                S.op("dve", lambda e, gb=gb, oc=oc: e.tensor_tensor(out=mT[:, oc, :], in0=gb[2][:, :], in1=gb[3][:, :], op=ALU.add), reads=[gk[2], gk[3]], writes=["mT"])
            if p3_limit <= 2:
                break
            for s in range(4):
                for nh in range(2):
                    b = 2 * s + nh
                    for kc in range(8):
                        S.op("pe", lambda e, kc=kc, s=s, nh=nh, b=b: e.matmul(bank(b), lhsT=mT[:, kc, s * 128:(s + 1) * 128], rhs=Wo_[:, kc, nh * 512:(nh + 1) * 512],
                                                                          start=(kc == 0), stop=(kc == 7)),
                             reads=["mT", "Wo"], writes=[pk(b)])
                i = xslots[s]
                post_norm_residual(s, gbcA1, "gbcA", xsl3[i][:, :], "xsl3_%d" % i, x1all[:, tt * 4 + s, :], ("x1", tt * 4 + s))
            if p3_limit <= 3:
                break

        if dbg:
            d_x1 = nc.dram_tensor("d_x1", [OWN, D], F32, kind="ExternalOutput").ap()
            S.op("sp", lambda e: e.dma_start(out=d_x1.rearrange("(a p) d -> p a d", p=128), in_=x1all[:, :, :]), reads=[("x1", i) for i in range(16)],
                 writes=[("dbg", 7)], akey="dbg7")

        WdnS = [sb("WdnS%d" % i, [128, 4, D], BF16, R0 + 8192 * i) for i in range(4)]
        uT = sb("uT", [128, 32, 512], BF16, R0 + 32768)
        WupS = [sb("WupS%d" % i, [128, 8, 512], BF16, R0 + 65536 + 8192 * i) for i in range(4)]
        gbcB0 = sb("gbcB0", [128, D], F32, T0 + 65536)
        gbcB1 = sb("gbcB1", [128, D], F32, T0 + 69632)
        rl = [sb("rl%d" % i, [128, 512], F32, T0 + 73728 + 2048 * i) for i in range(2)]
        SB_R = ["WdnS%d" % i for i in range(4)] + ["WupS%d" % i for i in range(4)] + [("uT", i) for i in range(32)]
        S.alias(SB_R, SA_R)
        S.alias(["gbcB", "rl0", "rl1"], ["xsl3_%d" % i for i in range(4)])
        S.op("sp", lambda e: e.dma_start(out=gbcB0[:, :], in_=gvec[2:3, :].partition_broadcast(128)), writes=["gbcB"], akey="gbcB", chain=True)
        S.op("sp", lambda e: e.dma_start(out=gbcB1[:, :], in_=gvec[3:4, :].partition_broadcast(128)), writes=["gbcB"], akey="gbcB", chain=True)

        seq = []
        for tt in range(4):
            seq += [("up", tt, g_) for g_ in range(8)] + [("dn", tt, g_) for g_ in range(8)]
        wptr = [0]
        slot_of = {}
        cnts = {"up": 0, "dn": 0}
        wup_v = wup.rearrange("(kc p) n -> p kc n", p=128)
        wdn_v = wdn.rearrange("(fc p) n -> p fc n", p=128)

        def ensure_loaded(n):
            while wptr[0] <= min(n, len(seq) - 1):
                kind, tt_, g_ = seq[wptr[0]]
                sl = cnts[kind] % 4
                cnts[kind] += 1
                slot_of[seq[wptr[0]]] = sl
                if kind == "up":
                    S.op("pool", lambda e, sl=sl, g_=g_: e.dma_start(out=WupS[sl][:, :, :], in_=wup_v[:, :, g_ * 512:(g_ + 1) * 512]),
                         writes=["WupS%d" % sl], akey="wup%d" % sl)
                else:
                    S.op("pool", lambda e, sl=sl, g_=g_: e.dma_start(out=WdnS[sl][:, :, :], in_=wdn_v[:, g_ * 4:(g_ + 1) * 4, :]),
                         writes=["WdnS%d" % sl], akey="wdn%d" % sl)
                wptr[0] += 1

        LOOK = 3
        rlc = [0]
        hT3b = sb("hT3b", [128, 8, 512], BF16, T0 + 100352)
        S.alias(["hT3b"], ["gate1_%d" % i for i in range(4)])
        hTB = [(hT3, "hT3"), (hT3b, "hT3b")]

        def mlp_front(tt_):
            ht, hk = hTB[tt_ % 2]
            for s in range(4):
                norm_transpose(x1all[:, tt_ * 4 + s, :], ("x1", tt_ * 4 + s), gbcB0, ["gbcB"], ht, hk, s, [0, 1, 2, 3], nb3)
            evac_transposes(ht, hk, [0, 1, 2, 3])

        for tt in range(4 if p3_limit >= 5 else 0):
            ensure_loaded(tt * 16 + LOOK)
            if tt == 0:
                mlp_front(0)
            hT3c, hT3k = hTB[tt % 2]
            for g_ in range(8):
                if g_ == 2 and tt + 1 < 4 and p3_limit >= 99:
                    mlp_front(tt + 1)
                n = tt * 16 + g_
                ensure_loaded(n + LOOK)
                sl = slot_of[("up", tt, g_)]
                for fcl in range(4):
                    fc = g_ * 4 + fcl
                    b = 4 + fc % 4
                    for kc in range(8):
                        S.op("pe", lambda e, kc=kc, fcl=fcl, b=b, sl=sl, hT3c=hT3c: e.matmul(bank(b), lhsT=WupS[sl][:, kc, fcl * 128:(fcl + 1) * 128], rhs=hT3c[:, kc, :],
                                                                              start=(kc == 0), stop=(kc == 7)),
                             reads=["WupS%d" % sl, hT3k], writes=[pk(b)])
                    ri = rlc[0] % 2
                    rlc[0] += 1
                    S.op("act", lambda e, b=b, ri=ri: e.activation(out=rl[ri][:, :], in_=bank(b), func=AF.Relu), reads=[pk(b)], writes=["rl%d" % ri])
                    S.op("dve", lambda e, fc=fc, ri=ri: e.tensor_tensor(out=uT[:, fc, :], in0=rl[ri][:, :], in1=rl[ri][:, :], op=ALU.mult),
                         reads=["rl%d" % ri], writes=[("uT", fc)])
            if p3_limit <= 5:
                break
            for g_ in range(8):
                n = tt * 16 + 8 + g_
                ensure_loaded(n + LOOK)
                sl = slot_of[("dn", tt, g_)]
                for s in range(4):
                    for nh in range(2):
                        b = 2 * s + nh
                        for fcl in range(4):
                            fc = g_ * 4 + fcl
                            S.op("pe", lambda e, fc=fc, fcl=fcl, s=s, nh=nh, b=b, sl=sl, g_=g_: e.matmul(
                                bank(b), lhsT=uT[:, fc, s * 128:(s + 1) * 128], rhs=WdnS[sl][:, fcl, nh * 512:(nh + 1) * 512],
                                start=(g_ == 0 and fcl == 0), stop=(g_ == 7 and fcl == 3)),
                                 reads=[("uT", fc), "WdnS%d" % sl], writes=[pk(b)])
            for s in (2, 3, 0, 1):
                idx = tt * 4 + s
                post_norm_residual(s, gbcB1, "gbcB", x1all[:, idx, :], ("x1", idx), x1all[:, idx, :], ("x1", idx))
                S.op("sp", lambda e, idx=idx: e.dma_start(out=out[idx * 128:(idx + 1) * 128, :], in_=x1all[:, idx, :]),
                     reads=[("x1", idx)], writes=[("out", idx)], akey="o%d" % (idx % 4))
            if p3_limit <= 6:
                break


    if stop_after < 3:
        S.op("sp", lambda e: e.dma_start(out=out[0:128, :], in_=xsl[0][:, :]), reads=["xsl0"], writes=["outdummy"], akey="outd")
    S.emit()
    return nc


def t5_bucket_np(rel):
    half = 16
    max_exact = 8
    ret = np.where(rel > 0, half, 0)
    n = np.abs(rel)
    nf = np.maximum(n, 1).astype(np.float32)
    large = max_exact + (np.log(nf / max_exact) / math.log(128 / max_exact) * (half - max_exact)).astype(np.int32)
    large = np.minimum(large, half - 1)
    return ret + np.where(n < max_exact, n, large)


def make_bias_tiles(rel_bias, g):
    kk = np.arange(128)[:, None]
    qq = np.arange(128)[None, :]
    bt = np.empty((128, 5, 4, 128), np.float32)
    for jj in range(5):
        j = jj - 1
        for r in range(4):
            delta = (j - r) * 128
            rel = delta + kk - qq
            vals = rel_bias[t5_bucket_np(rel), g]
            if j > r:
                allowed = np.zeros((128, 128), bool)
            elif j == r:
                allowed = kk < (qq // 64 + 1) * 64
            else:
                allowed = np.ones((128, 128), bool)
            bt[:, jj, r, :] = np.where(allowed, vals, np.float32(NEG))
    return np.ascontiguousarray(bt.reshape(128, 5 * 512))


def make_in_maps(inputs):
    x = np.asarray(inputs["x"], np.float32)
    w_in = np.asarray(inputs["w_in"], np.float32)[0]
    rel_bias = np.asarray(inputs["rel_bias"], np.float32)
    gvec = np.stack([np.asarray(inputs[k], np.float32)[0] for k in ("g_pre_mix", "g_post_mix", "g_pre_mlp", "g_post_mlp")])
    lamv = np.concatenate([np.asarray(inputs[k], np.float32)[0] for k in ("lambda_q1", "lambda_k1", "lambda_q2", "lambda_k2")])[None, :]
    subln = np.ascontiguousarray(np.asarray(inputs["w_subln"], np.float32)[0][:, None])
    shared = dict(
        wgate=np.ascontiguousarray(w_in[:, 3072:5120]),
        wsb=np.asarray(inputs["w_sb_out"], np.float32)[0],
        wdf=np.asarray(inputs["w_diff_out"], np.float32)[0],
        wo=np.asarray(inputs["w_o"], np.float32)[0],
        wup=np.asarray(inputs["w_up"], np.float32)[0],
        wdn=np.asarray(inputs["w_down"], np.float32)[0],
        gvec=np.ascontiguousarray(gvec), lamv=np.ascontiguousarray(lamv), subln=subln,
    )
    maps = []
    for c in range(8):
        b, g = c // 4, c % 4
        cols = np.concatenate([np.arange(0 + 128 * g, 0 + 128 * g + 128), np.arange(512 + 128 * g, 512 + 128 * g + 128),
                               np.arange(1536 + 128 * g, 1536 + 128 * g + 128), np.arange(2048 + 128 * g, 2048 + 128 * g + 128),
                               np.arange(1024 + 128 * g, 1024 + 128 * g + 128), np.arange(2560 + 128 * g, 2560 + 128 * g + 128)])
        oh = np.zeros((128, 4), np.float32)
        oh[:, g] = 1.0
        m = dict(shared)
        m.update(
            xb=np.ascontiguousarray(x[b]),
            xs=np.ascontiguousarray(x[b, g * OWN:(g + 1) * OWN]),
            wqkv=np.ascontiguousarray(w_in[:, cols]),
            bt=make_bias_tiles(rel_bias, g),
            c15=np.full((128, 1), rel_bias[15, g], np.float32),
            onehot=oh,
        )
        maps.append(m)
    return maps


_NC_CACHE = {}


def kernel(**inputs):
    if "nc" not in _NC_CACHE:
        _NC_CACHE["nc"] = build()
    nc = _NC_CACHE["nc"]
    maps = make_in_maps(inputs)
    res = run_bass_kernel_spmd(nc, maps, core_ids=list(range(8)))
    outp = np.empty((2, SEQ, D), np.float32)
    for c in range(8):
        b, g = c // 4, c % 4
        outp[b, g * OWN:(g + 1) * OWN] = res.results[c]["out"]
    return outp
```

```python
import math
import numpy as np
import ml_dtypes
import concourse.bass as bass
import concourse.mybir as mybir
from concourse.bass_utils import run_bass_kernel_spmd

F32 = mybir.dt.float32
BF16 = mybir.dt.bfloat16
AF = mybir.ActivationFunctionType
ALU = mybir.AluOpType
AX = mybir.AxisListType

SEQ = 8192
D = 1024
DFF = 4096
NQT = 16
NKB = 64
OWN = 2048
EPS = 1e-6
NEG = -30000.0
LAM_INIT = 0.8 - 0.6 * math.exp(-0.3 * 0)
ENGS = ("pe", "act", "dve", "pool", "sp")


class Op:
    __slots__ = ("eng", "fn", "deps", "is_async", "inc", "sem", "val", "signaled", "noinline")

    def __init__(self, eng, fn, is_async, inc):
        self.eng = eng
        self.fn = fn
        self.deps = []
        self.is_async = is_async
        self.inc = inc
        self.sem = None
        self.val = 0
        self.signaled = is_async
        self.noinline = False


class Sched:
    def __init__(self, nc):
        self.nc = nc
        self.ops = {e: [] for e in ENGS}
        self.last_w = {}
        self.readers = {}
        self.akeys = {}

    def alias(self, new_keys, old_keys):
        olds = []
        seen = set()
        for k in old_keys:
            w = self.last_w.get(k)
            for o in ([w] if w is not None else []) + list(self.readers.get(k, ())):
                if id(o) not in seen:
                    seen.add(id(o))
                    olds.append(o)
        for k in new_keys:
            self.last_w[k] = None
            self.readers[k] = list(olds)

    def op(self, eng, fn, reads=(), writes=(), akey=None, inc=16, chain=False, noinline=False):
        o = Op(eng, fn, akey is not None, inc)
        o.noinline = noinline
        sem = ("a", akey) if akey is not None else None
        deps = []
        for k in reads:
            w = self.last_w.get(k)
            if w is not None:
                deps.append(w)
        for k in writes:
            w = self.last_w.get(k)
            if w is not None and not (chain and w.sem == sem):
                deps.append(w)
            deps.extend(self.readers.get(k, ()))
        seen = set()
        for d in deps:
            if id(d) in seen:
                continue
            seen.add(id(d))
            if (not d.is_async) and d.eng == eng and eng == "pe":
                continue
            o.deps.append(d)
            d.signaled = True
        if akey is not None:
            o.sem = sem
            self.akeys[akey] = self.akeys.get(akey, 0) + inc
            o.val = self.akeys[akey]
        for k in writes:
            self.last_w[k] = o
            self.readers[k] = []
        for k in reads:
            if k not in writes:
                lst = self.readers.setdefault(k, [])
                if not o.is_async:
                    lst[:] = [r for r in lst if r.is_async or r.eng != eng]
                lst.append(o)
        self.ops[eng].append(o)
        return o

    def emit(self):
        nc = self.nc
        for e in ENGS:
            c = 0
            for o in self.ops[e]:
                if o.is_async:
                    continue
                if o.signaled:
                    c += 1
                    o.sem = ("e", e)
                    o.val = c
        names = [("e", e) for e in ENGS] + [("a", k) for k in self.akeys]
        sems = {}
        for i, sn in enumerate(names):
            sems[sn] = nc.alloc_semaphore(name="s%d" % i)
        self.n_sems = len(names)
        final = {("a", k): v for k, v in self.akeys.items()}
        ops = self.ops

        def run_stream(e, engine):
            waited = {}
            for o in ops[e]:
                need = {}
                for d in o.deps:
                    if d.val > waited.get(d.sem, 0) and d.val > need.get(d.sem, 0):
                        need[d.sem] = d.val
                items = list(need.items())
                inl = None
                if items and e in ("act", "dve", "pool") and not o.is_async and not o.noinline:
                    inl = items.pop()
                for sm, v in items:
                    engine.wait_ge(sems[sm], v)
                    waited[sm] = v
                ins = o.fn(engine)
                if inl is not None:
                    ins.wait_op(sems[inl[0]], inl[1], "sem-ge")
                    waited[inl[0]] = inl[1]
                if o.is_async:
                    ins.then_inc(sems[o.sem], o.inc)
                elif o.signaled:
                    ins.then_inc(sems[o.sem], 1)
            if e == "sp":
                for k, v in final.items():
                    if waited.get(k, 0) < v:
                        engine.wait_ge(sems[k], v)

        with nc.Block() as block:
            @block.tensor
            def _(eng):
                run_stream("pe", eng)

            @block.scalar
            def _(eng):
                run_stream("act", eng)

            @block.vector
            def _(eng):
                run_stream("dve", eng)

            @block.gpsimd
            def _(eng):
                run_stream("pool", eng)

            @block.sync
            def _(eng):
                run_stream("sp", eng)


def build(stop_after=3, dbg=False, skip12=False, p3_limit=99):
    nc = bass.Bass("TRN2", target_bir_lowering=False)
    S = Sched(nc)

    def din(name, shape, dt=F32):
        return nc.dram_tensor(name, list(shape), dt, kind="ExternalInput").ap()

    xb = din("xb", [SEQ, D])
    xs = din("xs", [OWN, D])
    wqkv = din("wqkv", [D, 768])
    wgate = din("wgate", [D, 2 * D])
    wsb = din("wsb", [512, D])
    wdf = din("wdf", [512, D])
    wo = din("wo", [D, D])
    wup = din("wup", [D, DFF])
    wdn = din("wdn", [DFF, D])
    gvec = din("gvec", [4, D])
    lamv = din("lamv", [1, 256])
    subln = din("subln", [128, 1])
    btd = din("bt", [128, 5 * 512])
    c15d = din("c15", [128, 1])
    ohd = din("onehot", [128, 4])
    out = nc.dram_tensor("out", [OWN, D], F32, kind="ExternalOutput").ap()
    rs_in_l = [nc.dram_tensor("rs_in0", [4 * 1024, 1536], BF16), nc.dram_tensor("rs_in1", [4 * 1024, 512], BF16)]
    rs_out_l = [nc.dram_tensor("rs_out0", [1024, 1536], BF16), nc.dram_tensor("rs_out1", [1024, 512], BF16)]
    GROUPS = [[qt for qt in range(NQT) if qt % 4 != 3], [qt for qt in range(NQT) if qt % 4 == 3]]
    cur = {"gi": 0}
    if dbg:
        d_qk = nc.dram_tensor("d_qk", [4, 128, SEQ], BF16, kind="ExternalOutput").ap()
        d_v = nc.dram_tensor("d_v", [2, 128, NKB * 128], BF16, kind="ExternalOutput").ap()
        d_rs = nc.dram_tensor("d_rs", [1024, OWN], BF16, kind="ExternalOutput").ap()

    arena = nc.alloc_sbuf_tensor("arena", [128, 212832], mybir.dt.uint8)
    ABASE = 16512
    assert nc.sbuf_base >= ABASE + 212832 - 64, (nc.sbuf_base,)

    def sb(name, shape, dt, off):
        return nc.alloc_sbuf_tensor_at(name, list(shape), dt, offset=ABASE + off)

    ps = nc.alloc_psum_tensor("ps", [128, 8 * 512], F32)
    psb16 = ps.bitcast(BF16)

    def bank(i, lo=0, hi=512, p0=0, p1=128):
        return ps[p0:p1, i * 512 + lo:i * 512 + hi]

    def pk(i):
        return ("pb", i)

    R0, C0, T0 = 0, 98304, 100352
    QT_sb = sb("QT_sb", [128, SEQ], BF16, R0 + 0)
    KT_sb = sb("KT_sb", [128, SEQ], BF16, R0 + 16384)
    QT_d = sb("QT_d", [128, SEQ], BF16, R0 + 32768)
    QT_sbB = sb("QT_sbB", [128, SEQ], BF16, T0 + 77824)
    QT_dB = sb("QT_dB", [128, SEQ], BF16, T0 + 94208)
    QTs = [QT_sb, QT_sbB]
    QTd = [QT_d, QT_dB]
    KT_d = sb("KT_d", [128, SEQ], BF16, R0 + 49152)
    V_sb = sb("V_sb", [128, NKB, 128], BF16, R0 + 65536)
    V_d = sb("V_d", [128, NKB, 128], BF16, R0 + 81920)
    ident = sb("ident", [128, 128], BF16, C0 + 0)
    uneg = sb("uneg", [128, 128], BF16, C0 + 256)
    ones = sb("ones", [128, 128], BF16, C0 + 512)
    onesm = sb("onesm", [128, 128], BF16, C0 + 768)
    mtri = sb("mtri", [128, 128], BF16, C0 + 1024)
    onehot = sb("onehot_sb", [128, 4], F32, C0 + 1280)
    c15 = sb("c15_sb", [128, 1], F32, C0 + 1312)
    sublnS = sb("subln_sb", [128, 1], F32, C0 + 1344)
    lamS = sb("lam_sb", [128, 8], F32, C0 + 1376)
    stat = sb("stat", [128, 32], F32, C0 + 1408)
    cf32 = sb("cf32", [128, 128], F32, C0 + 1536)

    def acts(fn, reads, writes):
        return S.op("act", fn, reads, writes)

    def build_const(dst, val, sel):
        key = "cf32"
        S.op("pool", lambda e: e.memset(cf32[:, :], val), writes=[key])
        if sel is not None:
            S.op("pool", lambda e: e.affine_select(out=cf32[:, :], in_=cf32[:, :], pattern=[[-1, 128]],
                                                   compare_op=sel, fill=0.0, base=0, channel_multiplier=1),
                 reads=[key], writes=[key])
        S.op("dve", lambda e: e.tensor_copy(out=dst[:, :], in_=cf32[:, :]), reads=[key], writes=[dst.name])

    QZ = ["qz0", "qz1", "qz2", "qz3"]
    build_const(ident, 1.0, ALU.is_equal)
    build_const(uneg, -1.0, ALU.is_ge)
    build_const(mtri, NEG, ALU.is_ge)
    build_const(onesm, 1.0 / 128.0, None)
    build_const(ones, 1.0, None)

    xsl = [sb("xsl%d" % i, [128, D], F32, T0 + 4096 * i) for i in range(4)]
    junk = sb("junk", [128, D], BF16, T0 + 16384)
    xn = [sb("xn%d" % i, [128, D], BF16, T0 + 18432 + 2048 * i) for i in range(2)]
    hT = [sb("hT%d" % i, [128, 8, 512], BF16, T0 + 22528 + 8192 * i) for i in range(2)]
    wq = sb("wq", [128, 8, 768], BF16, T0 + 38912)
    gbc0 = sb("gbc0", [128, D], F32, T0 + 51200)
    lamB = sb("lamB", [128, 256], F32, T0 + 55296)
    lamP = sb("lamP", [128, 128], F32, T0 + 56320)
    P1_KEYS = ["xsl%d" % i for i in range(4)] + ["junk", "xn0", "xn1", "hT0", "hT1", "wq", "gbc0", "lamB", "lamP"] + ["xnq%d" % i for i in range(8)]

    cst = "consts"
    S.op("sp", lambda e: e.dma_start(out=onehot[:, :], in_=ohd[:, :]), writes=[cst], akey=cst, chain=True)
    S.op("sp", lambda e: e.dma_start(out=c15[:, :], in_=c15d[:, :]), writes=[cst], akey=cst, chain=True)
    S.op("sp", lambda e: e.dma_start(out=sublnS[:, :], in_=subln[:, :]), writes=[cst], akey=cst, chain=True)
    S.op("sp", lambda e: e.dma_start(out=gbc0[:, :], in_=gvec[0:1, :].partition_broadcast(128)), writes=[cst], akey=cst, chain=True)
    S.op("sp", lambda e: e.dma_start(out=lamB[:, :], in_=lamv[0:1, :].partition_broadcast(128)), writes=[cst], akey=cst, chain=True)
    S.op("pool", lambda e: e.dma_start(out=wq[:, :, :], in_=wqkv.rearrange("(kc p) n -> p kc n", p=128)),
         writes=["wq"], akey="wq")
    S.op("pool", lambda e: e.memset(QT_sb[64:128, :], 0.0), writes=["qz0"])
    S.op("pool", lambda e: e.memset(QT_sbB[0:64, :], 0.0), writes=["qz1"])
    S.op("pool", lambda e: e.memset(QT_d[64:128, :], 0.0), writes=["qz2"])
    S.op("pool", lambda e: e.memset(QT_dB[0:64, :], 0.0), writes=["qz3"])

    S.op("dve", lambda e: e.tensor_tensor(out=lamP[:, 0:64], in0=lamB[:, 0:64], in1=lamB[:, 64:128], op=ALU.mult), reads=[cst], writes=["lamP"])
    S.op("dve", lambda e: e.tensor_tensor(out=lamP[:, 64:128], in0=lamB[:, 128:192], in1=lamB[:, 192:256], op=ALU.mult), reads=[cst, "lamP"], writes=["lamP"])
    S.op("dve", lambda e: e.reduce_sum(out=lamS[:, 0:1], in_=lamP[:, 0:64], axis=AX.X), reads=["lamP"], writes=["lamS"])
    S.op("dve", lambda e: e.reduce_sum(out=lamS[:, 1:2], in_=lamP[:, 64:128], axis=AX.X), reads=["lamP", "lamS"], writes=["lamS"])
    S.op("act", lambda e: e.activation(out=lamS[:, 2:4], in_=lamS[:, 0:2], func=AF.Exp), reads=["lamS"], writes=["lamS"])
    S.op("dve", lambda e: e.tensor_tensor(out=lamS[:, 4:5], in0=lamS[:, 3:4], in1=lamS[:, 2:3], op=ALU.subtract), reads=["lamS"], writes=["lamS"])
    S.op("dve", lambda e: e.tensor_scalar(out=lamS[:, 5:6], in0=lamS[:, 4:5], scalar1=-LAM_INIT, scalar2=None, op0=ALU.add), reads=["lamS"], writes=["lamS"])
    S.op("dve", lambda e: e.tensor_scalar(out=sublnS[:, :], in0=sublnS[:, :], scalar1=1.0 - LAM_INIT, scalar2=None, op0=ALU.mult), reads=[cst], writes=["sublnS"])
    neglam = lamS[:, 5:6]

    cnt = {"stat": 0, "xn": 0}

    def norm_transpose(x_ap, xkey, gbc, gkeys, hT_t, hkey, s, tr_banks, nb=None):
        if nb is None:
            nb = (junk, "junk", xn, ["xn0", "xn1"])
        junk_t, junk_k, xn_l, xn_k = nb
        c = (cnt["stat"] % 8) * 4
        cnt["stat"] += 1
        skey = ("stat", c)
        S.op("act", lambda e: e.activation(out=junk_t[:, :], in_=x_ap, func=AF.Square, accum_out=stat[:, c:c + 1]),
             reads=[xkey], writes=[junk_k, skey], noinline=True)
        S.op("act", lambda e: e.activation(out=stat[:, c + 1:c + 2], in_=stat[:, c:c + 1], func=AF.Ln, scale=1.0 / D, bias=EPS),
             reads=[skey], writes=[skey])
        S.op("act", lambda e: e.activation(out=stat[:, c + 2:c + 3], in_=stat[:, c + 1:c + 2], func=AF.Exp, scale=-0.5),
             reads=[skey], writes=[skey])
        xi = cnt["xn"] % 2
        cnt["xn"] += 1
        xt = xn_l[xi]
        S.op("dve", lambda e: e.scalar_tensor_tensor(out=xt[:, :], in0=x_ap, scalar=stat[:, c + 2:c + 3], in1=gbc[:, :],
                                                     op0=ALU.mult, op1=ALU.mult),
             reads=[xkey, skey] + gkeys, writes=[xn_k[xi]])
        for kc in range(8):
            bi = tr_banks[kc // 2]
            o_ap = psb16[:, bi * 1024 + (kc % 2) * 512 + s * 128: bi * 1024 + (kc % 2) * 512 + (s + 1) * 128]
            S.op("pe", lambda e, o_ap=o_ap, kc=kc: e.transpose(o_ap, xt[:, kc * 128:(kc + 1) * 128], ident[:, :]),
                 reads=[xn_k[xi], "ident"], writes=[pk(bi)])

    def norm_only(x_ap, xkey, gbc, gkeys, xt, xk, junk_t, junk_k):
        c = (cnt["stat"] % 8) * 4
        cnt["stat"] += 1
        skey = ("stat", c)
        S.op("act", lambda e: e.activation(out=junk_t[:, :], in_=x_ap, func=AF.Square, accum_out=stat[:, c:c + 1]),
             reads=[xkey], writes=[junk_k, skey], noinline=True)
        S.op("act", lambda e: e.activation(out=stat[:, c + 1:c + 2], in_=stat[:, c:c + 1], func=AF.Ln, scale=1.0 / D, bias=EPS),
             reads=[skey], writes=[skey])
        S.op("act", lambda e: e.activation(out=stat[:, c + 2:c + 3], in_=stat[:, c + 1:c + 2], func=AF.Exp, scale=-0.5),
             reads=[skey], writes=[skey])
        S.op("dve", lambda e: e.scalar_tensor_tensor(out=xt[:, :], in0=x_ap, scalar=stat[:, c + 2:c + 3], in1=gbc[:, :],
                                                     op0=ALU.mult, op1=ALU.mult),
             reads=[xkey, skey] + gkeys, writes=[xk])

    def tr_only(xt, xk, s, tr_banks):
        for kc in range(8):
            bi = tr_banks[kc // 2]
            o_ap = psb16[:, bi * 1024 + (kc % 2) * 512 + s * 128: bi * 1024 + (kc % 2) * 512 + (s + 1) * 128]
            S.op("pe", lambda e, o_ap=o_ap, kc=kc: e.transpose(o_ap, xt[:, kc * 128:(kc + 1) * 128], ident[:, :]),
                 reads=[xk, "ident"], writes=[pk(bi)])

    def evac_transposes(hT_t, hkey, tr_banks):
        for i, bi in enumerate(tr_banks):
            src = psb16[:, bi * 1024:(bi + 1) * 1024]
            dst = hT_t[:, 2 * i:2 * i + 2, :]
            if i % 2 == 0:
                S.op("act", lambda e, src=src, dst=dst: e.copy(out=dst, in_=src.rearrange("p (a n) -> p a n", a=2)),
                     reads=[pk(bi)], writes=[hkey])
            else:
                S.op("dve", lambda e, src=src, dst=dst: e.tensor_copy(out=dst, in_=src.rearrange("p (a n) -> p a n", a=2)),
                     reads=[pk(bi)], writes=[hkey])

    xcnt = [0]

    def load_x(src_ap):
        i = xcnt[0] % 4
        xcnt[0] += 1
        S.op("sp", lambda e: e.dma_start(out=xsl[i][:, :], in_=src_ap), writes=["xsl%d" % i], akey="xsl%d" % i)
        return i

    xnq = [sb("xnq%d" % i, [128, D], BF16, T0 + 57344 + 2048 * i) for i in range(8)]

    def p1_norm(t, subs=(0, 1, 2, 3)):
        for s in subs:
            i = load_x(xb[(t * 4 + s) * 128:(t * 4 + s + 1) * 128, :])
            q = (t % 2) * 4 + s
            norm_only(xsl[i][:, :], "xsl%d" % i, gbc0, [cst], xnq[q], "xnq%d" % q, junk, "junk")

    def p1_front(t):
        slot = t % 2
        for s in range(4):
            q = (t % 2) * 4 + s
            tr_only(xnq[q], "xnq%d" % q, s, [0, 1, 2, 3])
        evac_transposes(hT[slot], "hT%d" % slot, [0, 1, 2, 3])

    def p1_back(t, between=None):
        slot = t % 2
        hk = "hT%d" % slot
        dsts = [(QTs, "QT_sb", 0.125), (KT_sb, "KT_sb", None), (QTd, "QT_d", 0.125), (KT_d, "KT_d", None)]
        for cg in range(4):
            if between is not None:
                between(cg)
            bi = 4 + cg % 2
            for kc in range(8):
                S.op("pe", lambda e, cg=cg, kc=kc, bi=bi: e.matmul(bank(bi), lhsT=wq[:, kc, cg * 128:(cg + 1) * 128], rhs=hT[slot][:, kc, :],
                                                              start=(kc == 0), stop=(kc == 7)),
                     reads=["wq", hk], writes=[pk(bi)])
            dst, dk, sc = dsts[cg]
            if sc is not None:
                for hh in range(2):
                    d_ap = dst[hh][64 * hh:64 * hh + 64, t * 512:(t + 1) * 512]
                    S.op("act", lambda e, d_ap=d_ap, bi=bi, sc=sc, hh=hh: e.activation(out=d_ap, in_=bank(bi, 0, 512, 64 * hh, 64 * hh + 64), func=AF.Copy, scale=sc),
                         reads=[pk(bi)] + QZ, writes=[(dk, t)])
            else:
                d_ap = dst[:, t * 512:(t + 1) * 512]
                S.op("dve", lambda e, d_ap=d_ap, bi=bi: e.tensor_copy(out=d_ap, in_=bank(bi)),
                     reads=[pk(bi)], writes=[(dk, t)])
        for s in range(4):
            bi = 6 + s % 2
            for kc in range(8):
                S.op("pe", lambda e, s=s, kc=kc, bi=bi: e.matmul(bank(bi, 0, 256), lhsT=hT[slot][:, kc, s * 128:(s + 1) * 128], rhs=wq[:, kc, 512:768],
                                                            start=(kc == 0), stop=(kc == 7)),
                     reads=["wq", hk], writes=[pk(bi)])
            kb = t * 4 + s
            if s % 2 == 0:
                S.op("dve", lambda e, kb=kb, bi=bi: e.tensor_copy(out=V_sb[:, kb, :], in_=bank(bi, 0, 128)), reads=[pk(bi)], writes=[("V_sb", kb)])
                S.op("dve", lambda e, kb=kb, bi=bi: e.tensor_copy(out=V_d[:, kb, :], in_=bank(bi, 128, 256)), reads=[pk(bi)], writes=[("V_d", kb)])
            else:
                S.op("act", lambda e, kb=kb, bi=bi: e.copy(out=V_sb[:, kb, :], in_=bank(bi, 0, 128)), reads=[pk(bi)], writes=[("V_sb", kb)])
                S.op("act", lambda e, kb=kb, bi=bi: e.copy(out=V_d[:, kb, :], in_=bank(bi, 128, 256)), reads=[pk(bi)], writes=[("V_d", kb)])

    for t in range(NQT + 1):
        if skip12:
            break
        if t == 0:
            p1_norm(0)
        if t < NQT:
            p1_front(t)
        nxt = (lambda cg, t=t: p1_norm(t + 1, (cg,))) if t + 1 < NQT else None
        if t >= 1:
            p1_back(t - 1, nxt)
        elif nxt is not None:
            for cg in range(4):
                nxt(cg)

    R_ATT = [(n, t) for n in ("QT_sb", "KT_sb", "QT_d", "KT_d") for t in range(NQT)] + \
            [(n, kb) for n in ("V_sb", "V_d") for kb in range(NKB)]

    if dbg:
        for i, (tn, nm) in enumerate(((QT_sb, "QT_sb"), (KT_sb, "KT_sb"), (QT_d, "QT_d"), (KT_d, "KT_d"))):
            S.op("sp", lambda e, i=i, tn=tn: e.dma_start(out=d_qk[i], in_=tn[:, :]), reads=[(nm, t) for t in range(NQT)], writes=[("dbg", i)], akey="dbg%d" % i)
        S.op("sp", lambda e: e.dma_start(out=d_v[0], in_=V_sb[:, :, :].rearrange("p a b -> p (a b)")), reads=[("V_sb", k) for k in range(NKB)], writes=[("dbg", 4)], akey="dbg4")
        S.op("sp", lambda e: e.dma_start(out=d_v[1], in_=V_d[:, :, :].rearrange("p a b -> p (a b)")), reads=[("V_d", k) for k in range(NKB)], writes=[("dbg", 5)], akey="dbg5")

    if stop_after >= 2 and not skip12:
        eB = [sb("eB%d" % i, [128, 512], F32, T0 + 2048 * i) for i in range(2)]
        LpB = [sb("LpB%d" % i, [128, 512], BF16, T0 + 4096 + 1024 * i) for i in range(2)]
        argB = [sb("argB%d" % i, [128, 512], F32, T0 + 6144 + 2048 * i) for i in range(2)]
        wB = [sb("wB%d" % i, [128, 512], BF16, T0 + 10240 + 1024 * i) for i in range(2)]
        carry = sb("carry", [128, 512], F32, T0 + 12288)
        stg = [[sb("stg%d_%d" % (s_, i), [128, 512], BF16, T0 + 14336 + 4096 * s_ + 1024 * i) for i in range(4)] for s_ in range(2)]
        aB = [[sb("aB%d_%d" % (s_, m), [128, 512], F32, T0 + 22528 + 4096 * s_ + 2048 * m) for m in range(2)] for s_ in range(2)]
        dwB = [[sb("dwB%d_%d" % (s_, m), [128, 512], BF16, T0 + 30720 + 2048 * s_ + 1024 * m) for m in range(2)] for s_ in range(2)]
        BT = sb("BT", [128, 5 * 512], F32, T0 + 34816)
        pvs = [sb("pvs%d" % m, [128, 512], F32, T0 + 45056 + 2048 * m) for m in range(2)]
        lsS = [sb("lsS%d" % m, [128, 512], F32, T0 + 49152 + 2048 * m) for m in range(2)]
        rB = [sb("rB%d" % m, [128, 512], F32, T0 + 53248 + 2048 * m) for m in range(2)]
        t1B = sb("t1B", [128, 512], F32, T0 + 57344)
        yvB = sb("yvB", [128, 512], F32, T0 + 59392)
        ysqB = sb("ysqB", [128, 512], BF16, T0 + 61440)
        lnvB = sb("lnvB", [128, 512], F32, T0 + 62464)
        rstB = sb("rstB", [128, 512], F32, T0 + 64512)
        ynB = sb("ynB", [128, 512], F32, T0 + 66560)
        dstg = [[sb("dstg%d_%d" % (s_, i), [128, 512], BF16, T0 + 68608 + 4096 * s_ + 1024 * i) for i in range(4)] for s_ in range(2)]
        P2_KEYS = (["eB0", "eB1", "LpB0", "LpB1", "argB0", "argB1", "wB0", "wB1", "carry", "BT",
                    "pvs0", "pvs1", "lsS0", "lsS1", "rB0", "rB1", "t1B", "yvB", "ysqB", "lnvB", "rstB", "ynB"]
                   + ["stg%d_%d" % (s_, i) for s_ in range(2) for i in range(4)]
                   + ["dstg%d_%d" % (s_, i) for s_ in range(2) for i in range(4)]
                   + ["aB%d_%d" % (s_, m) for s_ in range(2) for m in range(2)]
                   + ["dwB%d_%d" % (s_, m) for s_ in range(2) for m in range(2)])
        P2_KEYS = P2_KEYS + ["aP0", "aP1", "dwP0", "dwP1", ("wacc", 0), ("wacc", 1), "eP0", "eP1", "LpP0", "LpP1", "wP0", "wP1", "carryP", "sbev"] + ["stgP%d" % i for i in range(4)]
        S.alias(P2_KEYS, P1_KEYS)
        S.op("sp", lambda e: e.dma_start(out=BT[:, :], in_=btd[:, :]), writes=["BT"], akey="bt")

        rs_keys = []

        tiles = []

        eP = [sb("eP%d" % i, [128, 2, 512], F32, T0 + 4096 * i) for i in range(2)]
        LpP = [sb("LpP%d" % i, [128, 2, 512], BF16, T0 + 8192 + 2048 * i) for i in range(2)]
        wP = [sb("wP%d" % i, [128, 2, 512], BF16, T0 + 12288 + 2048 * i) for i in range(2)]
        stgP = [sb("stgP%d" % i, [128, 512], BF16, T0 + 16384 + 1024 * i) for i in range(4)]
        carryP = sb("carryP", [128, 2, 512], F32, T0 + 53248)
        sbev = sb("sbev", [128, 512], F32, T0 + 20480)

        def psP(b0, cs):
            return ps[:, b0 * 512:(b0 + 2) * 512].rearrange("p (m n) -> p m n", m=2)[:, :, cs:512]

        def sb_z(i):
            T = tiles[i]
            s_ = i % 2
            cs = T["cs"]
            q0 = T["qt"] * 512
            k0 = T["kb"] * 128
            diag = T["j"] >= 0
            for h in range(2):
                S.op("pe", lambda e, h=h: e.matmul(bank(h, cs, 512), lhsT=KT_sb[:, k0:k0 + 128], rhs=QTs[h][:, q0 + cs:q0 + 512],
                                                   start=True, stop=not diag),
                     reads=[("KT_sb", T["kb"] // 4), ("QT_sb", T["qt"])], writes=[pk(h)])
                if diag:
                    S.op("pe", lambda e, h=h: e.matmul(bank(h, cs, cs + 128), lhsT=ident[:, :], rhs=mtri[:, :], start=False, stop=True),
                         reads=["ident", "mtri"], writes=[pk(h)])
            S.op("act", lambda e: e.activation(out=eP[s_][:, :, cs:512], in_=psP(0, cs), func=AF.Exp),
                 reads=[pk(0), pk(1)], writes=["eP%d" % s_])
            S.op("act", lambda e: e.activation(out=LpP[s_][:, :, cs:512], in_=eP[s_][:, :, cs:512], func=AF.Ln, bias=1.0, scale=1.0),
                 reads=["eP%d" % s_], writes=["LpP%d" % s_])

        def sb_mid(i):
            T = tiles[i]
            s_ = i % 2
            cs = T["cs"]
            q0 = T["qt"] * 512
            k0 = T["kb"] * 128
            diag = T["j"] >= 0
            for h in range(2):
                bB = 2 + h
                S.op("pe", lambda e, h=h, bB=bB: e.matmul(bank(bB, cs, 512), lhsT=uneg[:, :], rhs=LpP[s_][:, h, cs:512], start=True, stop=False),
                     reads=["uneg", "LpP%d" % s_], writes=[pk(bB)])
                S.op("pe", lambda e, h=h, bB=bB: e.matmul(bank(bB, cs, 512), lhsT=KT_sb[:, k0:k0 + 128], rhs=QTs[h][:, q0 + cs:q0 + 512],
                                                          start=False, stop=not diag),
                     reads=[("KT_sb", T["kb"] // 4), ("QT_sb", T["qt"])], writes=[pk(bB)])
                if diag:
                    S.op("pe", lambda e, bB=bB: e.matmul(bank(bB, cs, cs + 128), lhsT=ident[:, :], rhs=mtri[:, :], start=False, stop=True),
                         reads=["ident", "mtri"], writes=[pk(bB)])
            if not T["last"]:
                for h in range(2):
                    bC = 4 + h
                    S.op("pe", lambda e, h=h, bC=bC: e.matmul(bank(bC, cs, 512), lhsT=ones[:, :], rhs=LpP[s_][:, h, cs:512], start=True, stop=True),
                         reads=["ones", "LpP%d" % s_], writes=[pk(bC)])
            if T["first"]:
                S.op("dve", lambda e: e.memset(carryP[:, :, :], 0.0), writes=["carryP"])
            S.op("dve", lambda e: e.tensor_tensor(out=eP[s_][:, :, cs:512], in0=psP(2, cs), in1=carryP[:, :, cs:512], op=ALU.subtract),
                 reads=[pk(2), pk(3), "carryP"], writes=["eP%d" % s_])
            if not T["last"]:
                S.op("dve", lambda e: e.tensor_tensor(out=carryP[:, :, cs:512], in0=psP(4, cs), in1=carryP[:, :, cs:512], op=ALU.add),
                     reads=[pk(4), pk(5), "carryP"], writes=["carryP"])
            S.op("act", lambda e: e.activation(out=wP[s_][:, :, cs:512], in_=eP[s_][:, :, cs:512], func=AF.Exp),
                 reads=["eP%d" % s_], writes=["wP%d" % s_])

        def sb_pv(i):
            T = tiles[i]
            s_ = i % 2
            cs = T["cs"]
            kb = T["kb"]
            qt = T["qt"]
            for h in range(2):
                S.op("pe", lambda e, h=h: e.matmul(bank(6 + h, cs, 512), lhsT=V_sb[:, kb, :], rhs=wP[s_][:, h, cs:512],
                                                   start=T["first"], stop=T["last"], skip_group_check=True),
                     reads=[("V_sb", kb), "wP%d" % s_], writes=[pk(6 + h)])
            if T["last"]:
                for h in range(2):
                    S.op("dve", lambda e, h=h: e.tensor_copy(out=sbev[64 * h:64 * h + 64, :], in_=bank(6 + h, 0, 512, 64 * h, 64 * h + 64)),
                         reads=[pk(6 + h)] + (["sbev"] if h == 1 else []), writes=["sbev"])
                for i4 in range(4):
                    st = stgP[i4]
                    skey = "stgP%d" % i4
                    S.op("pool", lambda e, st=st, i4=i4: e.tensor_scalar(out=st[:, :], in0=sbev[:, :], scalar1=onehot[:, i4:i4 + 1], scalar2=None, op0=ALU.mult),
                         reads=["sbev", cst], writes=[skey])
                    r0 = (qt // 4) * 1024 + 128 * i4
                    c0 = (qt % 4) * 512 if cur["gi"] == 0 else 0
                    rin = rs_in_l[cur["gi"]]
                    rk = ("rs_in", 0, qt, i4)
                    rs_keys.append(rk)
                    S.op("sp", lambda e, st=st, r0=r0, c0=c0, rin=rin: e.dma_start(out=rin[r0:r0 + 128, c0:c0 + 512], in_=st[:, :]),
                         reads=[skey], writes=[rk], akey=skey)

        dt_tiles = []
        dcnt = [0]

        waccT = sb("waccT", [128, 2, 512], F32, T0 + 53248)
        aP = [sb("aP%d" % s_, [128, 2, 512], F32, T0 + 22528 + 4096 * s_) for s_ in range(2)]
        dwP = [sb("dwP%d" % s_, [128, 2, 512], BF16, T0 + 30720 + 2048 * s_) for s_ in range(2)]

        def pair_ps(s_, cs):
            return ps[:, (2 * s_) * 512:(2 * s_ + 2) * 512].rearrange("p (m n) -> p m n", m=2)[:, :, cs:512]

        def df_z(i):
            T = dt_tiles[i]
            s_ = i % 2
            cs = T["cs"]
            q0 = T["qt"] * 512
            k0 = T["kb"] * 128
            wk = "dwP%d" % s_
            for m in range(2):
                bz = 2 * s_ + m
                S.op("pe", lambda e, m=m, bz=bz: e.matmul(bank(bz, cs, 512), lhsT=KT_d[:, k0:k0 + 128],
                                                          rhs=QTd[m][:, q0 + cs:q0 + 512], start=True, stop=True),
                     reads=[("KT_d", T["kb"] // 4), ("QT_d", T["qt"])], writes=[pk(bz)])
            if T["near"]:
                jj = T["j"] + 1
                ak = "aP%d" % s_
                for m in range(2):
                    bz = 2 * s_ + m
                    S.op("dve", lambda e, m=m, bz=bz, jj=jj: e.tensor_tensor(out=aP[s_][:, m, cs:512], in0=bank(bz, cs, 512),
                                                                         in1=BT[:, jj * 512 + cs:(jj + 1) * 512], op=ALU.add),
                         reads=[pk(bz), "BT"] + ([ak] if m == 1 else []), writes=[ak])
                S.op("act", lambda e: e.activation(out=dwP[s_][:, :, cs:512], in_=aP[s_][:, :, cs:512], func=AF.Exp),
                     reads=[ak], writes=[wk])
            else:
                S.op("act", lambda e: e.activation(out=dwP[s_][:, :, :], in_=pair_ps(s_, 0), func=AF.Exp, bias=c15[:, 0:1], scale=1.0),
                     reads=[pk(2 * s_), pk(2 * s_ + 1), cst], writes=[wk])

        dpend = []

        def df_flush(force):
            for ent in list(dpend):
                ent[0] -= 1
                if force or ent[0] <= 0:
                    dpend.remove(ent)
                    ent[1]()

        def df_pv(i):
            T = dt_tiles[i]
            s_ = i % 2
            cs = T["cs"]
            kb = T["kb"]
            qt = T["qt"]
            wk = "dwP%d" % s_
            df_flush(T["last"])
            for m in range(2):
                S.op("pe", lambda e, m=m: e.matmul(bank(4 + m, cs, 512), lhsT=V_d[:, kb, :], rhs=dwP[s_][:, m, cs:512],
                                                   start=T["first"], stop=T["last"], skip_group_check=True),
                     reads=[("V_d", kb), wk], writes=[pk(4 + m)])
                S.op("pe", lambda e, m=m: e.matmul(bank(6 + m, cs, 512), lhsT=ones[:, :], rhs=dwP[s_][:, m, cs:512],
                                                   start=T["first"], stop=T["last"], skip_group_check=True),
                     reads=["ones", wk], writes=[pk(6 + m)])
            if T["last"]:
                S.op("dve", lambda e: e.tensor_copy(out=pvs[0][:, :], in_=bank(4)), reads=[pk(4)], writes=["pvs0"])
                S.op("dve", lambda e: e.tensor_copy(out=pvs[1][:, :], in_=bank(5)), reads=[pk(5)], writes=["pvs1"])
                S.op("act", lambda e: e.copy(out=lsS[0][:, :], in_=bank(6)), reads=[pk(6)], writes=["lsS0"])
                S.op("act", lambda e: e.copy(out=lsS[1][:, :], in_=bank(7)), reads=[pk(7)], writes=["lsS1"])
                for m in range(2):
                    S.op("dve", lambda e, m=m: e.reciprocal(out=lsS[m][:, :], in_=lsS[m][:, :]), reads=["lsS%d" % m], writes=["lsS%d" % m])
                S.op("dve", lambda e: e.tensor_tensor(out=t1B[:, :], in0=pvs[0][:, :], in1=lsS[0][:, :], op=ALU.mult), reads=["pvs0", "lsS0"], writes=["t1B"])
                S.op("dve", lambda e: e.tensor_tensor(out=pvs[1][:, :], in0=pvs[1][:, :], in1=lsS[1][:, :], op=ALU.mult), reads=["pvs1", "lsS1"], writes=["pvs1"])
                S.op("dve", lambda e: e.scalar_tensor_tensor(out=yvB[:, :], in0=pvs[1][:, :], scalar=neglam, in1=t1B[:, :], op0=ALU.mult, op1=ALU.add),
                     reads=["pvs1", "t1B", "lamS"], writes=["yvB"])
                S.op("dve", lambda e: e.tensor_tensor(out=ysqB[:, :], in0=yvB[:, :], in1=yvB[:, :], op=ALU.mult), reads=["yvB"], writes=["ysqB"])
                gi_now = cur["gi"]
                dpend.append([7, lambda qt=qt, gi_now=gi_now: df_tail(qt, gi_now)])

        def df_tail(qt, gi_now):
                S.op("pe", lambda e: e.matmul(bank(0), lhsT=onesm[:, :], rhs=ysqB[:, :], start=True, stop=True), reads=["onesm", "ysqB"], writes=[pk(0)])
                S.op("act", lambda e: e.activation(out=lnvB[:, :], in_=bank(0), func=AF.Ln, bias=EPS, scale=1.0), reads=[pk(0)], writes=["lnvB"])
                S.op("act", lambda e: e.activation(out=rstB[:, :], in_=lnvB[:, :], func=AF.Exp, scale=-0.5), reads=["lnvB"], writes=["rstB"])
                S.op("dve", lambda e: e.scalar_tensor_tensor(out=ynB[:, :], in0=yvB[:, :], scalar=sublnS[:, 0:1], in1=rstB[:, :], op0=ALU.mult, op1=ALU.mult),
                     reads=["yvB", "rstB", "sublnS"], writes=["ynB"])
                ss_ = dcnt[0] % 2
                dcnt[0] += 1
                for i4 in range(4):
                    st = dstg[ss_][i4]
                    skey = "dstg%d_%d" % (ss_, i4)
                    S.op("pool", lambda e, st=st, i4=i4: e.tensor_scalar(out=st[:, :], in0=ynB[:, :], scalar1=onehot[:, i4:i4 + 1], scalar2=None, op0=ALU.mult),
                         reads=["ynB", cst], writes=[skey])
                    r0 = (qt // 4) * 1024 + 512 + 128 * i4
                    c0 = (qt % 4) * 512 if gi_now == 0 else 0
                    rin = rs_in_l[gi_now]
                    rk = ("rs_in", 2, qt, i4)
                    rs_keys.append(rk)
                    S.op("sp", lambda e, st=st, r0=r0, c0=c0, rin=rin: e.dma_start(out=rin[r0:r0 + 128, c0:c0 + 512], in_=st[:, :]),
                         reads=[skey], writes=[rk], akey=skey)

        Wg = sb("Wg", [128, 8, 2 * D], BF16, R0 + 0)
        Wo_ = sb("Wo", [128, 8, D], BF16, R0 + 65536)
        for gi, qts in enumerate(GROUPS):
            cur["gi"] = gi
            rs_keys = []
            tiles = []
            for qt in qts:
                for kb in range(4 * qt + 3, -1, -1):
                    j = kb - 4 * qt
                    tiles.append(dict(qt=qt, kb=kb, j=j, cs=max(j, 0) * 128, first=(kb == 4 * qt + 3), last=(kb == 0)))
            N = len(tiles)
            for t in range(N + 2):
                if t < N:
                    sb_z(t)
                if 1 <= t <= N:
                    sb_mid(t - 1)
                if t >= 2:
                    sb_pv(t - 2)
            if gi == len(GROUPS) - 1 and stop_after >= 3:
                S.alias(["Wg"], [(n, t) for n in ("QT_sb", "KT_sb") for t in range(NQT)])
                for kc in range(8):
                    S.op("pool", lambda e, kc=kc: e.dma_start(out=Wg[:, kc, :], in_=wgate[kc * 128:(kc + 1) * 128, :]), writes=["Wg"], akey="Wg", chain=True)
                S.alias(["Wo"], [("V_sb", kb) for kb in range(NKB)])
                for kc in range(8):
                    S.op("pool", lambda e, kc=kc: e.dma_start(out=Wo_[:, kc, :], in_=wo[kc * 128:(kc + 1) * 128, :]), writes=["Wo"], akey="Wo", chain=True)
            dt_tiles = []
            for qt in qts:
                for kb in range(0, 4 * qt + 4):
                    j = kb - 4 * qt
                    dt_tiles.append(dict(qt=qt, kb=kb, j=j, cs=max(j, 0) * 128, first=(kb == 0), last=(kb == 4 * qt + 3), near=(j >= -1)))
            ND = len(dt_tiles)
            for t in range(ND + 1):
                if t < ND:
                    df_z(t)
                if t >= 1:
                    df_pv(t - 1)
            df_flush(True)
            S.op("pool", lambda e, gi=gi: e.collective_compute("ReduceScatter", ALU.add, replica_groups=[[0, 1, 2, 3], [4, 5, 6, 7]],
                                                               ins=[rs_in_l[gi].ap().opt()], outs=[rs_out_l[gi].ap().opt()]),
                 reads=rs_keys, writes=["rs_out%d" % gi], akey="cc%d" % gi, inc=1)
        if dbg:
            S.op("pool", lambda e: e.dma_start(out=d_rs[:, 0:1536], in_=rs_out_l[0][:, :]), reads=["rs_out0"], writes=[("dbg", 6)], akey="dbg6")
            S.op("pool", lambda e: e.dma_start(out=d_rs[:, 1536:2048], in_=rs_out_l[1][:, :]), reads=["rs_out1"], writes=[("dbg", 8)], akey="dbg8")

    if stop_after >= 3:
        Wsb_ = sb("Wsb", [128, 4, D], BF16, R0 + 81920)
        Wdf_ = sb("Wdf", [128, 4, D], BF16, R0 + 90112)
        yT = sb("yT", [128, 8, 512], BF16, R0 + 32768)
        mT = sb("mT", [128, 8, 512], BF16, R0 + 40960)
        gate = [[sb("gate%d_%d" % (p_, i), [128, 512], F32, (R0 + 49152 if p_ == 0 else T0 + 100352) + 2048 * i) for i in range(4)] for p_ in range(2)]
        gbcA0 = sb("gbcA0", [128, D], F32, R0 + 57344)
        gbcA1 = sb("gbcA1", [128, D], F32, R0 + 61440)
        x1all = sb("x1all", [128, 16, D], F32, T0 + 0)
        xsl3 = [sb("xsl3_%d" % i, [128, D], F32, T0 + 65536 + 4096 * i) for i in range(4)]
        junk3 = sb("junk3", [128, D], BF16, T0 + 81920)
        xn3 = [sb("xn3_%d" % i, [128, D], BF16, T0 + 83968 + 2048 * i) for i in range(2)]
        hT3 = sb("hT3", [128, 8, 512], BF16, T0 + 88064)
        tmp3 = sb("tmp3", [128, D], F32, T0 + 96256)
        nb3 = (junk3, "junk3", xn3, ["xn3_0", "xn3_1"])
        SA_R = ["Wg", "Wsb", "Wdf", "Wo", "yT", "mT", "gbcA"] + ["gate0_%d" % i for i in range(4)]
        S.alias([k for k in SA_R if k not in ("Wg", "Wo")], R_ATT)
        if skip12:
            Wg = sb("Wg", [128, 8, 2 * D], BF16, R0 + 0)
            Wo_ = sb("Wo", [128, 8, D], BF16, R0 + 65536)
            for kc in range(8):
                S.op("pool", lambda e, kc=kc: e.dma_start(out=Wg[:, kc, :], in_=wgate[kc * 128:(kc + 1) * 128, :]), writes=["Wg"], akey="Wg", chain=True)
            for kc in range(8):
                S.op("pool", lambda e, kc=kc: e.dma_start(out=Wo_[:, kc, :], in_=wo[kc * 128:(kc + 1) * 128, :]), writes=["Wo"], akey="Wo", chain=True)
        P3_T = ([("x1", i) for i in range(16)] + ["xsl3_%d" % i for i in range(4)] + ["junk3", "xn3_0", "xn3_1", "hT3", "tmp3"]
                + ["gate1_%d" % i for i in range(4)])
        S.alias(P3_T, (P2_KEYS + R_ATT) if not skip12 else [])

        for kc in range(4):
            S.op("pool", lambda e, kc=kc: e.dma_start(out=Wsb_[:, kc, :], in_=wsb[kc * 128:(kc + 1) * 128, :]), writes=["Wsb"], akey="Wsb", chain=True)
        for kc in range(4):
            S.op("pool", lambda e, kc=kc: e.dma_start(out=Wdf_[:, kc, :], in_=wdf[kc * 128:(kc + 1) * 128, :]), writes=["Wdf"], akey="Wdf", chain=True)
        S.op("sp", lambda e: e.dma_start(out=gbcA0[:, :], in_=gvec[0:1, :].partition_broadcast(128)), writes=["gbcA"], akey="gbcA", chain=True)
        S.op("sp", lambda e: e.dma_start(out=gbcA1[:, :], in_=gvec[1:2, :].partition_broadcast(128)), writes=["gbcA"], akey="gbcA", chain=True)

        x3cnt = [0]

        def load_x3(src_ap):
            i = x3cnt[0] % 4
            x3cnt[0] += 1
            S.op("sp", lambda e: e.dma_start(out=xsl3[i][:, :], in_=src_ap), writes=["xsl3_%d" % i], akey="xsl%d" % i)
            return i

        def post_norm_residual(p_, gbc, gkey, res_ap, res_key, dst_ap, dst_key):
            c = (cnt["stat"] % 8) * 4
            cnt["stat"] += 1
            skey = ("stat", c)
            zap = ps[:, 2 * p_ * 512:(2 * p_ + 2) * 512]
            S.op("act", lambda e: e.activation(out=junk3[:, :], in_=zap, func=AF.Square, accum_out=stat[:, c:c + 1]),
                 reads=[pk(2 * p_), pk(2 * p_ + 1)], writes=["junk3", skey], noinline=True)
            S.op("act", lambda e: e.activation(out=stat[:, c + 1:c + 2], in_=stat[:, c:c + 1], func=AF.Ln, scale=1.0 / D, bias=EPS),
                 reads=[skey], writes=[skey])
            S.op("act", lambda e: e.activation(out=stat[:, c + 2:c + 3], in_=stat[:, c + 1:c + 2], func=AF.Exp, scale=-0.5),
                 reads=[skey], writes=[skey])
            S.op("dve", lambda e: e.scalar_tensor_tensor(out=tmp3[:, :], in0=zap, scalar=stat[:, c + 2:c + 3], in1=gbc[:, :], op0=ALU.mult, op1=ALU.mult),
                 reads=[pk(2 * p_), pk(2 * p_ + 1), skey, gkey], writes=["tmp3"])
            S.op("dve", lambda e: e.tensor_tensor(out=dst_ap, in0=tmp3[:, :], in1=res_ap, op=ALU.add),
                 reads=["tmp3", res_key], writes=[dst_key])

        for tt in range(4):
            rsi, rc0 = (0, tt * 512) if tt < 3 else (1, 0)
            S.op("sp", lambda e, rsi=rsi, rc0=rc0: e.dma_start(out=yT[:, :, :], in_=rs_out_l[rsi].ap().rearrange("(kc p) n -> p kc n", p=128)[:, :, rc0:rc0 + 512]),
                 reads=["rs_out%d" % rsi], writes=["yT"], akey="yT")
            xslots = []
            for s in range(4):
                i = load_x3(xs[(tt * 4 + s) * 128:(tt * 4 + s + 1) * 128, :])
                xslots.append(i)
                norm_transpose(xsl3[i][:, :], "xsl3_%d" % i, gbcA0, ["gbcA"], hT3, "hT3", s, [0, 1, 2, 3], nb3)
            evac_transposes(hT3, "hT3", [0, 1, 2, 3])
            if p3_limit <= 1:
                break
            for oc in range(8):
                p_ = oc % 2
                b4 = [4 * p_ + i for i in range(4)]
                gk = ["gate%d_%d" % (p_, i) for i in range(4)]
                gb = gate[p_]
                for kc in range(8):
                    S.op("pe", lambda e, kc=kc, oc=oc, b=b4[2]: e.matmul(bank(b), lhsT=Wg[:, kc, oc * 128:(oc + 1) * 128], rhs=hT3[:, kc, :], start=(kc == 0), stop=(kc == 7)),
                         reads=["Wg", "hT3"], writes=[pk(b4[2])])
                for kc in range(8):
                    S.op("pe", lambda e, kc=kc, oc=oc, b=b4[3]: e.matmul(bank(b), lhsT=Wg[:, kc, D + oc * 128:D + (oc + 1) * 128], rhs=hT3[:, kc, :], start=(kc == 0), stop=(kc == 7)),
                         reads=["Wg", "hT3"], writes=[pk(b4[3])])
                for kc in range(4):
                    S.op("pe", lambda e, kc=kc, oc=oc, b=b4[0]: e.matmul(bank(b), lhsT=Wsb_[:, kc, oc * 128:(oc + 1) * 128], rhs=yT[:, kc, :], start=(kc == 0), stop=(kc == 3)),
                         reads=["Wsb", "yT"], writes=[pk(b4[0])])
                for kc in range(4):
                    S.op("pe", lambda e, kc=kc, oc=oc, b=b4[1]: e.matmul(bank(b), lhsT=Wdf_[:, kc, oc * 128:(oc + 1) * 128], rhs=yT[:, 4 + kc, :], start=(kc == 0), stop=(kc == 3)),
                         reads=["Wdf", "yT"], writes=[pk(b4[1])])
                S.op("act", lambda e, gb=gb, b=b4[2]: e.activation(out=gb[0][:, :], in_=bank(b), func=AF.Sigmoid), reads=[pk(b4[2])], writes=[gk[0]])
                S.op("act", lambda e, gb=gb, b=b4[3]: e.activation(out=gb[1][:, :], in_=bank(b), func=AF.Sigmoid), reads=[pk(b4[3])], writes=[gk[1]])
                S.op("dve", lambda e, gb=gb, b=b4[0]: e.tensor_tensor(out=gb[2][:, :], in0=bank(b), in1=gb[0][:, :], op=ALU.mult), reads=[pk(b4[0]), gk[0]], writes=[gk[2]])
                S.op("dve", lambda e, gb=gb, b=b4[1]: e.tensor_tensor(out=gb[3][:, :], in0=bank(b), in1=gb[1][:, :], op=ALU.mult), reads=[pk(b4[1]), gk[1]], writes=[gk[3]])
                S.op("dve", lambda e, gb=gb, oc=oc: e.tensor_tensor(out=mT[:, oc, :], in0=gb[2][:, :], in1=gb[3][:, :], op=ALU.add), reads=[gk[2], gk[3]], writes=["mT"])
            if p3_limit <= 2:
                break
            for s in range(4):
                for nh in range(2):
                    b = 2 * s + nh
                    for kc in range(8):
                        S.op("pe", lambda e, kc=kc, s=s, nh=nh, b=b: e.matmul(bank(b), lhsT=mT[:, kc, s * 128:(s + 1) * 128], rhs=Wo_[:, kc, nh * 512:(nh + 1) * 512],
                                                                          start=(kc == 0), stop=(kc == 7)),
                             reads=["mT", "Wo"], writes=[pk(b)])
                i = xslots[s]
                post_norm_residual(s, gbcA1, "gbcA", xsl3[i][:, :], "xsl3_%d" % i, x1all[:, tt * 4 + s, :], ("x1", tt * 4 + s))
            if p3_limit <= 3:
                break

        if dbg:
            d_x1 = nc.dram_tensor("d_x1", [OWN, D], F32, kind="ExternalOutput").ap()
            S.op("sp", lambda e: e.dma_start(out=d_x1.rearrange("(a p) d -> p a d", p=128), in_=x1all[:, :, :]), reads=[("x1", i) for i in range(16)],
                 writes=[("dbg", 7)], akey="dbg7")

        WdnS = [sb("WdnS%d" % i, [128, 4, D], BF16, R0 + 8192 * i) for i in range(4)]
        uT = sb("uT", [128, 32, 512], BF16, R0 + 32768)
        WupS = [sb("WupS%d" % i, [128, 8, 512], BF16, R0 + 65536 + 8192 * i) for i in range(4)]
        gbcB0 = sb("gbcB0", [128, D], F32, T0 + 65536)
        gbcB1 = sb("gbcB1", [128, D], F32, T0 + 69632)
        rl = [sb("rl%d" % i, [128, 512], F32, T0 + 73728 + 2048 * i) for i in range(2)]
        SB_R = ["WdnS%d" % i for i in range(4)] + ["WupS%d" % i for i in range(4)] + [("uT", i) for i in range(32)]
        S.alias(SB_R, SA_R)
        S.alias(["gbcB", "rl0", "rl1"], ["xsl3_%d" % i for i in range(4)])
        S.op("sp", lambda e: e.dma_start(out=gbcB0[:, :], in_=gvec[2:3, :].partition_broadcast(128)), writes=["gbcB"], akey="gbcB", chain=True)
        S.op("sp", lambda e: e.dma_start(out=gbcB1[:, :], in_=gvec[3:4, :].partition_broadcast(128)), writes=["gbcB"], akey="gbcB", chain=True)

        seq = []
        for tt in range(4):
            seq += [("up", tt, g_) for g_ in range(8)] + [("dn", tt, g_) for g_ in range(8)]
        wptr = [0]
        slot_of = {}
        cnts = {"up": 0, "dn": 0}
        wup_v = wup.rearrange("(kc p) n -> p kc n", p=128)
        wdn_v = wdn.rearrange("(fc p) n -> p fc n", p=128)

        def ensure_loaded(n):
            while wptr[0] <= min(n, len(seq) - 1):
                kind, tt_, g_ = seq[wptr[0]]
                sl = cnts[kind] % 4
                cnts[kind] += 1
                slot_of[seq[wptr[0]]] = sl
                if kind == "up":
                    S.op("pool", lambda e, sl=sl, g_=g_: e.dma_start(out=WupS[sl][:, :, :], in_=wup_v[:, :, g_ * 512:(g_ + 1) * 512]),
                         writes=["WupS%d" % sl], akey="wup%d" % sl)
                else:
                    S.op("pool", lambda e, sl=sl, g_=g_: e.dma_start(out=WdnS[sl][:, :, :], in_=wdn_v[:, g_ * 4:(g_ + 1) * 4, :]),
                         writes=["WdnS%d" % sl], akey="wdn%d" % sl)
                wptr[0] += 1

        LOOK = 3
        rlc = [0]
        hT3b = sb("hT3b", [128, 8, 512], BF16, T0 + 100352)
        S.alias(["hT3b"], ["gate1_%d" % i for i in range(4)])
        hTB = [(hT3, "hT3"), (hT3b, "hT3b")]

        def mlp_front(tt_):
            ht, hk = hTB[tt_ % 2]
            for s in range(4):
                norm_transpose(x1all[:, tt_ * 4 + s, :], ("x1", tt_ * 4 + s), gbcB0, ["gbcB"], ht, hk, s, [0, 1, 2, 3], nb3)
            evac_transposes(ht, hk, [0, 1, 2, 3])

        for tt in range(4 if p3_limit >= 5 else 0):
            ensure_loaded(tt * 16 + LOOK)
            if tt == 0:
                mlp_front(0)
            hT3c, hT3k = hTB[tt % 2]
            for g_ in range(8):
                if g_ == 2 and tt + 1 < 4 and p3_limit >= 99:
                    mlp_front(tt + 1)
                n = tt * 16 + g_
                ensure_loaded(n + LOOK)
                sl = slot_of[("up", tt, g_)]
                for fcl in range(4):
                    fc = g_ * 4 + fcl
                    b = 4 + fc % 4
                    for kc in range(8):
                        S.op("pe", lambda e, kc=kc, fcl=fcl, b=b, sl=sl, hT3c=hT3c: e.matmul(bank(b), lhsT=WupS[sl][:, kc, fcl * 128:(fcl + 1) * 128], rhs=hT3c[:, kc, :],
                                                                              start=(kc == 0), stop=(kc == 7)),
                             reads=["WupS%d" % sl, hT3k], writes=[pk(b)])
                    ri = rlc[0] % 2
                    rlc[0] += 1
                    S.op("act", lambda e, b=b, ri=ri: e.activation(out=rl[ri][:, :], in_=bank(b), func=AF.Relu), reads=[pk(b)], writes=["rl%d" % ri])
                    S.op("dve", lambda e, fc=fc, ri=ri: e.tensor_tensor(out=uT[:, fc, :], in0=rl[ri][:, :], in1=rl[ri][:, :], op=ALU.mult),
                         reads=["rl%d" % ri], writes=[("uT", fc)])
            if p3_limit <= 5:
                break
            for g_ in range(8):
                n = tt * 16 + 8 + g_
                ensure_loaded(n + LOOK)
                sl = slot_of[("dn", tt, g_)]
                for s in range(4):
                    for nh in range(2):
                        b = 2 * s + nh
                        for fcl in range(4):
                            fc = g_ * 4 + fcl
                            S.op("pe", lambda e, fc=fc, fcl=fcl, s=s, nh=nh, b=b, sl=sl, g_=g_: e.matmul(
                                bank(b), lhsT=uT[:, fc, s * 128:(s + 1) * 128], rhs=WdnS[sl][:, fcl, nh * 512:(nh + 1) * 512],
                                start=(g_ == 0 and fcl == 0), stop=(g_ == 7 and fcl == 3)),
                                 reads=[("uT", fc), "WdnS%d" % sl], writes=[pk(b)])
            for s in (2, 3, 0, 1):
                idx = tt * 4 + s
                post_norm_residual(s, gbcB1, "gbcB", x1all[:, idx, :], ("x1", idx), x1all[:, idx, :], ("x1", idx))
                S.op("sp", lambda e, idx=idx: e.dma_start(out=out[idx * 128:(idx + 1) * 128, :], in_=x1all[:, idx, :]),
                     reads=[("x1", idx)], writes=[("out", idx)], akey="o%d" % (idx % 4))
            if p3_limit <= 6:
                break


    if stop_after < 3:
        S.op("sp", lambda e: e.dma_start(out=out[0:128, :], in_=xsl[0][:, :]), reads=["xsl0"], writes=["outdummy"], akey="outd")
    S.emit()
    return nc


def t5_bucket_np(rel):
    half = 16
    max_exact = 8
    ret = np.where(rel > 0, half, 0)
    n = np.abs(rel)
    nf = np.maximum(n, 1).astype(np.float32)
    large = max_exact + (np.log(nf / max_exact) / math.log(128 / max_exact) * (half - max_exact)).astype(np.int32)
    large = np.minimum(large, half - 1)
    return ret + np.where(n < max_exact, n, large)


def make_bias_tiles(rel_bias, g):
    kk = np.arange(128)[:, None]
    qq = np.arange(128)[None, :]
    bt = np.empty((128, 5, 4, 128), np.float32)
    for jj in range(5):
        j = jj - 1
        for r in range(4):
            delta = (j - r) * 128
            rel = delta + kk - qq
            vals = rel_bias[t5_bucket_np(rel), g]
            if j > r:
                allowed = np.zeros((128, 128), bool)
            elif j == r:
                allowed = kk < (qq // 64 + 1) * 64
            else:
                allowed = np.ones((128, 128), bool)
            bt[:, jj, r, :] = np.where(allowed, vals, np.float32(NEG))
    return np.ascontiguousarray(bt.reshape(128, 5 * 512))


def make_in_maps(inputs):
    x = np.asarray(inputs["x"], np.float32)
    w_in = np.asarray(inputs["w_in"], np.float32)[0]
    rel_bias = np.asarray(inputs["rel_bias"], np.float32)
    gvec = np.stack([np.asarray(inputs[k], np.float32)[0] for k in ("g_pre_mix", "g_post_mix", "g_pre_mlp", "g_post_mlp")])
    lamv = np.concatenate([np.asarray(inputs[k], np.float32)[0] for k in ("lambda_q1", "lambda_k1", "lambda_q2", "lambda_k2")])[None, :]
    subln = np.ascontiguousarray(np.asarray(inputs["w_subln"], np.float32)[0][:, None])
    shared = dict(
        wgate=np.ascontiguousarray(w_in[:, 3072:5120]),
        wsb=np.asarray(inputs["w_sb_out"], np.float32)[0],
        wdf=np.asarray(inputs["w_diff_out"], np.float32)[0],
        wo=np.asarray(inputs["w_o"], np.float32)[0],
        wup=np.asarray(inputs["w_up"], np.float32)[0],
        wdn=np.asarray(inputs["w_down"], np.float32)[0],
        gvec=np.ascontiguousarray(gvec), lamv=np.ascontiguousarray(lamv), subln=subln,
    )
    maps = []
    for c in range(8):
        b, g = c // 4, c % 4
        cols = np.concatenate([np.arange(0 + 128 * g, 0 + 128 * g + 128), np.arange(512 + 128 * g, 512 + 128 * g + 128),
                               np.arange(1536 + 128 * g, 1536 + 128 * g + 128), np.arange(2048 + 128 * g, 2048 + 128 * g + 128),
                               np.arange(1024 + 128 * g, 1024 + 128 * g + 128), np.arange(2560 + 128 * g, 2560 + 128 * g + 128)])
        oh = np.zeros((128, 4), np.float32)
        oh[:, g] = 1.0
        m = dict(shared)
        m.update(
            xb=np.ascontiguousarray(x[b]),
            xs=np.ascontiguousarray(x[b, g * OWN:(g + 1) * OWN]),
            wqkv=np.ascontiguousarray(w_in[:, cols]),
            bt=make_bias_tiles(rel_bias, g),
            c15=np.full((128, 1), rel_bias[15, g], np.float32),
            onehot=oh,
        )
        maps.append(m)
    return maps


_NC_CACHE = {}


def kernel(**inputs):
    if "nc" not in _NC_CACHE:
        _NC_CACHE["nc"] = build()
    nc = _NC_CACHE["nc"]
    maps = make_in_maps(inputs)
    res = run_bass_kernel_spmd(nc, maps, core_ids=list(range(8)))
    outp = np.empty((2, SEQ, D), np.float32)
    for c in range(8):
        b, g = c // 4, c % 4
        outp[b, g * OWN:(g + 1) * OWN] = res.results[c]["out"]
    return outp
```
